# Optimizing a Trainium2 kernel written in Bass

```python
import jax, jax.numpy as jnp
from jax import lax
import numpy as np

D_MODEL = 2048
BATCH = 4
SEQ = 2048
DEPTH = 4
DEC_BATCH = 128
DEC_SEQ = 4
PAST_LEN = 16384
PAGE_SIZE = 128

H_A = 8
DK = 128
DV = 128
W_QK = H_A * DK
W_A = H_A * DV
W_B = D_MODEL - W_A
LRU_BLOCKS = 8
LRU_BW = W_B // LRU_BLOCKS
LRU_C = 8.0
CONV_W = 4
CHUNK = 64
D_FF = ((8 * D_MODEL // 3 + 127) // 128) * 128
HALF = 0.5
EPS = 1e-6
OFF_Q = 0
OFF_K = OFF_Q + W_QK
OFF_V = OFF_K + W_QK
OFF_G = OFF_V + W_A
OFF_A = OFF_G + W_A
OFF_B = OFF_A + H_A
OFF_X = OFF_B + H_A
OFF_Y = OFF_X + W_B
N_IN = OFF_Y + W_B
N_DCONV = 2 * W_QK + W_A

kernel_name = "hymba_deltanet_rglru_macaron_step"


def rmsnorm(x, g):
    xf = x.astype(jnp.float32)
    y = xf * lax.rsqrt(jnp.mean(xf * xf, axis=-1, keepdims=True) + EPS)
    return (y * g.astype(jnp.float32)).astype(x.dtype)


def l2norm(t):
    return t * lax.rsqrt(jnp.sum(t * t, axis=-1, keepdims=True) + EPS)


def modulate(h, shift, scale):
    return h * (1 + scale[:, None, :]) + shift[:, None, :]


def swiglu(h, w1, w3, w2):
    return (jax.nn.silu(h @ w1) * (h @ w3)) @ w2


def causal_conv(x, prev, w):
    xp = jnp.concatenate([prev.astype(x.dtype), x], axis=1)
    L = x.shape[1]
    y = xp[:, 0:L] * w[0]
    for j in range(1, CONV_W):
        y = y + xp[:, j:j + L] * w[j]
    return y, xp[:, -(CONV_W - 1):]


def gated_delta_rule(q, k, v, g, beta, s0):
    B, L, H, _ = q.shape
    C = min(CHUNK, L)
    n = -(-L // C)
    pad = n * C - L

    def blocks(t):
        t = jnp.pad(t, [(0, 0), (0, pad)] + [(0, 0)] * (t.ndim - 2))
        t = t.reshape((B, n, C) + t.shape[2:])
        return jnp.moveaxis(t, 3, 1)

    qb, kb, vb, gb, bb = blocks(q), blocks(k), blocks(v), blocks(g), blocks(beta)
    gc = jnp.cumsum(gb, axis=-1)
    causal = jnp.tril(jnp.ones((C, C), dtype=bool))
    strict = jnp.tril(jnp.ones((C, C), dtype=bool), -1)
    decay = jnp.exp(jnp.where(causal, gc[..., :, None] - gc[..., None, :], -jnp.inf))
    kk = jnp.einsum('bhncd,bhnmd->bhncm', kb, kb)
    m = jnp.where(strict, bb[..., None] * kk * decay, 0.0)
    tri = jnp.eye(C, dtype=q.dtype) + m
    rhs = jnp.concatenate([kb * (bb * jnp.exp(gc))[..., None], vb * bb[..., None]], axis=-1)
    sol = lax.linalg.triangular_solve(tri, rhs, left_side=True, lower=True, unit_diagonal=True)
    w_blk, u_blk = sol[..., :DK], sol[..., DK:]
    qk = jnp.einsum('bhncd,bhnmd->bhncm', qb, kb) * decay
    q_dec = qb * jnp.exp(gc)[..., None]
    k_dec = kb * jnp.exp(gc[..., -1:] - gc)[..., None]
    g_last = jnp.exp(gc[..., -1])

    def step(S, xs):
        w_c, u_c, qk_c, qd_c, kd_c, gl_c = xs
        u = u_c - jnp.einsum('bhcd,bhde->bhce', w_c, S)
        o = jnp.einsum('bhcd,bhde->bhce', qd_c, S) + jnp.einsum('bhcm,bhme->bhce', qk_c, u)
        S = S * gl_c[..., None, None] + jnp.einsum('bhcd,bhce->bhde', kd_c, u)
        return S, o

    xs = tuple(jnp.moveaxis(t, 2, 0) for t in (w_blk, u_blk, qk, q_dec, k_dec, g_last))
    s_new, o = lax.scan(step, s0, xs)
    o = jnp.transpose(o, (1, 0, 3, 2, 4)).reshape(B, n * C, H, DV)[:, :L]
    return o, s_new


def delta_mixer(proj, conv_prev, s0, conv_w, a_log, dt_bias, onorm_g):
    B, L, _ = proj.shape
    f32 = jnp.float32
    qkv, conv_new = causal_conv(proj[..., OFF_Q:OFF_G], conv_prev, conv_w)
    qkv = jax.nn.silu(qkv.astype(f32))
    q = l2norm(qkv[..., OFF_Q:OFF_K].reshape(B, L, H_A, DK)) * (DK ** -0.5)
    k = l2norm(qkv[..., OFF_K:OFF_V].reshape(B, L, H_A, DK))
    v = qkv[..., OFF_V:OFF_G].reshape(B, L, H_A, DV)
    g = -jnp.exp(a_log.astype(f32)) * jax.nn.softplus(proj[..., OFF_A:OFF_B].astype(f32) + dt_bias.astype(f32))
    beta = jax.nn.sigmoid(proj[..., OFF_B:OFF_X].astype(f32))
    o, s_new = gated_delta_rule(q, k, v, g, beta, s0.astype(f32))
    gate = jax.nn.silu(proj[..., OFF_G:OFF_A].astype(f32)).reshape(B, L, H_A, DV)
    o = rmsnorm(o, onorm_g) * gate
    return o.reshape(B, L, W_A).astype(proj.dtype), conv_new, s_new.astype(s0.dtype)


def lru_mixer(proj, conv_prev, h0, conv_w, conv_b, w_a, b_a, w_x, b_x, lam):
    f32 = jnp.float32
    xc, conv_new = causal_conv(proj[..., OFF_X:OFF_Y], conv_prev, conv_w)
    xc = xc + conv_b
    B, L, _ = xc.shape
    xb = xc.reshape(B, L, LRU_BLOCKS, LRU_BW)
    r = jax.nn.sigmoid(jnp.einsum('blni,nij->blnj', xb, w_a).reshape(B, L, W_B).astype(f32) + b_a.astype(f32))
    i = jax.nn.sigmoid(jnp.einsum('blni,nij->blnj', xb, w_x).reshape(B, L, W_B).astype(f32) + b_x.astype(f32))
    log_a = -LRU_C * r * jax.nn.softplus(-lam.astype(f32))
    a = jnp.exp(log_a)
    b = jnp.sqrt(-jnp.expm1(2.0 * log_a)) * (i * xc.astype(f32))
    b = b.at[:, 0].add(a[:, 0] * h0.astype(f32))

    def comb(left, right):
        a1, b1 = left
        a2, b2 = right
        return a1 * a2, a2 * b1 + b2

    _, h = lax.associative_scan(comb, (a, b), axis=1)
    out = h * jax.nn.gelu(proj[..., OFF_Y:N_IN].astype(f32))
    return out.astype(proj.dtype), conv_new, h[:, -1].astype(h0.dtype)


def trunk(x, c, s_delta, s_dconv, s_lru, s_lconv, w_in, w_out, norm_g, w_ada, b_ada,
          ffn_w1, ffn_w3, ffn_w2, dconv_w, d_alog, d_dtbias, d_onorm,
          lconv_w, lconv_b, lru_wa, lru_ba, lru_wx, lru_bx, lru_lam, final_g):
    B = x.shape[0]
    cs = jax.nn.silu(c)
    new_d, new_dc, new_l, new_lc = [], [], [], []
    for l in range(DEPTH):
        mod = (cs @ w_ada[l] + b_ada[l]).reshape(B, 3, 3, D_MODEL).astype(x.dtype)
        h = modulate(rmsnorm(x, norm_g[l, 0]), mod[:, 0, 0], mod[:, 0, 1])
        x = x + HALF * mod[:, 0, 2][:, None] * swiglu(h, ffn_w1[l, 0], ffn_w3[l, 0], ffn_w2[l, 0])
        h = modulate(rmsnorm(x, norm_g[l, 1]), mod[:, 1, 0], mod[:, 1, 1])
        proj = h @ w_in[l]
        oa, dc, sd = delta_mixer(proj, s_dconv[l], s_delta[l], dconv_w[l], d_alog[l], d_dtbias[l], d_onorm[l])
        ob, lc, sl = lru_mixer(proj, s_lconv[l], s_lru[l], lconv_w[l], lconv_b[l],
                               lru_wa[l], lru_ba[l], lru_wx[l], lru_bx[l], lru_lam[l])
        x = x + mod[:, 1, 2][:, None] * (jnp.concatenate([oa, ob], axis=-1) @ w_out[l])
        h = modulate(rmsnorm(x, norm_g[l, 2]), mod[:, 2, 0], mod[:, 2, 1])
        x = x + HALF * mod[:, 2, 2][:, None] * swiglu(h, ffn_w1[l, 1], ffn_w3[l, 1], ffn_w2[l, 1])
        new_d.append(sd)
        new_dc.append(dc)
        new_l.append(sl)
        new_lc.append(lc)
    y = rmsnorm(x, final_g)
    return y, jnp.stack(new_d), jnp.stack(new_dc), jnp.stack(new_l), jnp.stack(new_lc)


def setup_inputs(seed: int = 0) -> dict:
    key = jax.random.key(seed)
    ks = jax.random.split(key, 32)
    f32 = jnp.float32
    nrm = lambda k, shape, s: jax.random.normal(k, shape, f32) * s
    D = D_MODEL
    dt = jnp.exp(jax.random.uniform(ks[20], (DEPTH, H_A), f32, np.log(1e-3), np.log(1e-1)))
    a_pow = jax.random.uniform(ks[21], (DEPTH, W_B), f32, 0.9, 0.999) ** (1.0 / LRU_C)
    return {
        'x_prompt': nrm(ks[0], (BATCH, SEQ, D), 1.0),
        'x_sample': nrm(ks[1], (DEC_BATCH, DEC_SEQ, D), 1.0),
        'c_prompt': nrm(ks[2], (BATCH, D), 1.0),
        'c_sample': nrm(ks[3], (DEC_BATCH, D), 1.0),
        'state_delta': nrm(ks[4], (DEPTH, DEC_BATCH, H_A, DK, DV), DK ** -0.5),
        'state_delta_conv': nrm(ks[5], (DEPTH, DEC_BATCH, CONV_W - 1, N_DCONV), 1.0),
        'state_lru': nrm(ks[6], (DEPTH, DEC_BATCH, W_B), 0.5),
        'state_lru_conv': nrm(ks[7], (DEPTH, DEC_BATCH, CONV_W - 1, W_B), 1.0),
        'w_in': nrm(ks[8], (DEPTH, D, N_IN), D ** -0.5),
        'w_out': nrm(ks[9], (DEPTH, W_A + W_B, D), (W_A + W_B) ** -0.5),
        'norm_g': 1.0 + nrm(ks[10], (DEPTH, 3, D), 0.05),
        'w_ada': nrm(ks[11], (DEPTH, D, 9 * D), 0.5 * D ** -0.5),
        'b_ada': nrm(ks[12], (DEPTH, 9 * D), 0.02),
        'ffn_w1': nrm(ks[13], (DEPTH, 2, D, D_FF), D ** -0.5),
        'ffn_w3': nrm(ks[14], (DEPTH, 2, D, D_FF), D ** -0.5),
        'ffn_w2': nrm(ks[15], (DEPTH, 2, D_FF, D), D_FF ** -0.5),
        'dconv_w': nrm(ks[16], (DEPTH, CONV_W, N_DCONV), CONV_W ** -0.5),
        'd_alog': jnp.log(jax.random.uniform(ks[17], (DEPTH, H_A), f32, 1.0, 16.0)),
        'd_dtbias': dt + jnp.log(-jnp.expm1(-dt)),
        'd_onorm': 1.0 + nrm(ks[18], (DEPTH, DV), 0.05),
        'lconv_w': nrm(ks[19], (DEPTH, CONV_W, W_B), CONV_W ** -0.5),
        'lconv_b': nrm(ks[22], (DEPTH, W_B), 0.02),
        'lru_wa': nrm(ks[23], (DEPTH, LRU_BLOCKS, LRU_BW, LRU_BW), LRU_BW ** -0.5),
        'lru_ba': nrm(ks[24], (DEPTH, W_B), 0.02),
        'lru_wx': nrm(ks[25], (DEPTH, LRU_BLOCKS, LRU_BW, LRU_BW), LRU_BW ** -0.5),
        'lru_bx': nrm(ks[26], (DEPTH, W_B), 0.02),
        'lru_lam': jnp.log(a_pow / (1.0 - a_pow)),
        'final_g': 1.0 + nrm(ks[27], (D,), 0.05),
    }


def reference(x_prompt, x_sample, c_prompt, c_sample, state_delta, state_delta_conv, state_lru, state_lru_conv,
              w_in, w_out, norm_g, w_ada, b_ada, ffn_w1, ffn_w3, ffn_w2, dconv_w, d_alog, d_dtbias, d_onorm,
              lconv_w, lconv_b, lru_wa, lru_ba, lru_wx, lru_bx, lru_lam, final_g):
    weights = (w_in, w_out, norm_g, w_ada, b_ada, ffn_w1, ffn_w3, ffn_w2, dconv_w, d_alog, d_dtbias, d_onorm,
               lconv_w, lconv_b, lru_wa, lru_ba, lru_wx, lru_bx, lru_lam, final_g)
    bp = x_prompt.shape[0]
    z_delta = jnp.zeros((DEPTH, bp, H_A, DK, DV), state_delta.dtype)
    z_dconv = jnp.zeros((DEPTH, bp, CONV_W - 1, N_DCONV), x_prompt.dtype)
    z_lru = jnp.zeros((DEPTH, bp, W_B), state_lru.dtype)
    z_lconv = jnp.zeros((DEPTH, bp, CONV_W - 1, W_B), x_prompt.dtype)
    y_prompt, p_delta, p_dconv, p_lru, p_lconv = trunk(x_prompt, c_prompt, z_delta, z_dconv, z_lru, z_lconv, *weights)
    y_sample, s_delta, s_dconv, s_lru, s_lconv = trunk(x_sample, c_sample, state_delta, state_delta_conv,
                                                       state_lru, state_lru_conv, *weights)
    return (y_prompt, y_sample, p_delta, p_dconv, p_lru, p_lconv, s_delta, s_dconv, s_lru, s_lconv)
```

```python
from contextlib import ExitStack

import numpy as np

import concourse.bass as bass
import concourse.mybir as mybir
from concourse.bass_utils import run_bass_kernel_spmd

F32 = mybir.dt.float32
BF16 = mybir.dt.bfloat16
AF = mybir.ActivationFunctionType
ALU = mybir.AluOpType
AX = mybir.AxisListType

D = 2048
KC = 16
H = 8
DK = 128
DV = 128
WB = 1024
NB = 8
DFF = 5504
FCN = 43
OFF_Q, OFF_K, OFF_V, OFF_G, OFF_A, OFF_B, OFF_X, OFF_Y, N_IN = 0, 1024, 2048, 3072, 4096, 4104, 4112, 5136, 6160
NDC = 3072
EPS = 1e-6
NSS = 16
LS = 4
NS = NSS * LS
NSLOT = 6
SLOT = 4096

ENGS = ("pe", "act", "dve", "pool", "sp")


def I(name, *args, **kw):
    return lambda e: getattr(e, name)(*args, **kw)


class Slot:
    def __init__(self, sem):
        self.sem = sem
        self.count = 0


class Op:
    __slots__ = ("eng", "fns", "deps", "marked", "rank", "dma")

    def __init__(self, eng, fns, deps, dma=None):
        self.eng, self.fns, self.deps, self.marked, self.rank, self.dma = eng, fns, deps, False, 0, dma


class Prog:
    def __init__(self):
        self.ops = {e: [] for e in ENGS}
        self.track = {}
        self.final = []

    def _touch(self, reg, is_write, ref, eng, deps):
        tid, lo, hi = reg
        if tid[0] == "p" and tid[1] == "s" and len(tid) == 3:
            lo, hi = 0, 2048
        lst = self.track.get(tid)
        if lst is None:
            lst = self.track[tid] = []
        new = []
        for rec in lst:
            rlo, rhi, rw, rref, reng = rec
            if rlo < hi and lo < rhi:
                if (is_write or rw) and rref is not ref:
                    deps.add(rref)
                if lo <= rlo and rhi <= hi and (is_write or ((not rw) and reng == eng)):
                    continue
            new.append(rec)
        new.append((lo, hi, is_write, ref, eng))
        self.track[tid] = new

    def add(self, eng, fns, reads=(), writes=()):
        if not isinstance(fns, (list, tuple)):
            fns = [fns]
        idx = len(self.ops[eng])
        ref = ("e", eng, idx)
        deps = set()
        for r in reads:
            self._touch(r, False, ref, eng, deps)
        for w in writes:
            self._touch(w, True, ref, eng, deps)
        op = Op(eng, list(fns), self._filter(eng, deps))
        self.ops[eng].append(op)
        return ref

    def _filter(self, eng, deps):
        out = []
        for d in deps:
            if d[0] == "e":
                if d[1] == "pe" and eng == "pe":
                    continue
                self.ops[d[1]][d[2]].marked = True
            out.append(d)
        return out

    def dma(self, eng, out_ap, in_ap, slot, reads=(), writes=(), final=False, slow=False):
        idx = len(self.ops[eng])
        prev = ("d", slot, slot.count) if slot.count else None
        slot.count += 16
        ref = ("d", slot, slot.count)
        deps = set()
        if prev is not None:
            deps.add(prev)
        for r in reads:
            self._touch(r, False, ref, ref, deps)
        for w in writes:
            self._touch(w, True, ref, ref, deps)
        kw = {"allow_slow_non_contiguous": True} if slow else {}
        op = Op(eng, [I("dma_start", out=out_ap, in_=in_ap, **kw)], self._filter("dma", deps), dma=slot)
        self.ops[eng].append(op)
        if final:
            self.final.append(ref)
        return ref

    def emit(self, nc, block, esem):
        for e in ENGS:
            r = 0
            for op in self.ops[e]:
                if op.marked:
                    r += 1
                    op.rank = r
        prog = self

        def resolve(d):
            if d[0] == "e":
                return esem[d[1]], prog.ops[d[1]][d[2]].rank
            return d[1].sem, d[2]

        def run(engname, engobj_name):
            def body(eng):
                waited = {}
                ops = prog.ops[engname]
                for op in ops:
                    need = {}
                    for d in op.deps:
                        sem, val = resolve(d)
                        k = id(sem)
                        if waited.get(k, 0) < val and need.get(k, (None, 0))[1] < val:
                            need[k] = (sem, val)
                    items = list(need.values())
                    for k, (sem, val) in need.items():
                        waited[k] = val
                    if op.dma is not None:
                        for sem, val in items:
                            eng.wait_ge(sem, val)
                        op.fns[0](eng).then_inc(op.dma.sem, 16)
                        continue
                    for sem, val in items[:-1]:
                        eng.wait_ge(sem, val)
                    last = None
                    for i, fn in enumerate(op.fns):
                        ins = fn(eng)
                        if i == 0 and items:
                            ins._wait_ge(items[-1][0], items[-1][1])
                        last = ins
                    if op.marked:
                        last.then_inc(esem[engname], 1)
                if engname == "sp":
                    for d in prog.final:
                        sem, val = resolve(d)
                        eng.wait_ge(sem, val)
            return body

        block.tensor(run("pe", None))
        block.scalar(run("act", None))
        block.vector(run("dve", None))
        block.gpsimd(run("pool", None))
        block.sync(run("sp", None))


class TT:
    def __init__(self, h, name, shape, esz):
        self.h, self.name, self.shape, self.esz = h, name, tuple(shape), esz
        st = []
        s = 1
        for n in reversed(self.shape[1:]):
            st.append(s)
            s *= n
        self.strides = tuple(reversed(st))

    def v(self, *idx, p=None):
        key = [slice(None) if p is None else slice(p[0], p[1])]
        lo = hi = 0
        for i, n in enumerate(self.shape[1:]):
            ix = idx[i] if i < len(idx) else None
            if ix is None:
                a, b = 0, n
                key.append(slice(None))
            elif isinstance(ix, tuple):
                a, b = ix
                key.append(slice(a, b))
            else:
                a, b = ix, ix + 1
                key.append(ix)
            lo += a * self.strides[i]
            hi += (b - 1) * self.strides[i]
        return self.h[tuple(key)], (self.name, lo * self.esz, (hi + 1) * self.esz)


class Arena:
    def __init__(self, h, words):
        self.h, self.words = h, words

    def f32(self, off, shape, p=None):
        n = int(np.prod(shape))
        assert off + n <= self.words, (off, n, self.words)
        ps = slice(None) if p is None else slice(p[0], p[1])
        ap = self.h[ps, off:off + n]
        if len(shape) == 2:
            ap = ap.rearrange("p (a b) -> p a b", b=shape[1])
        elif len(shape) == 3:
            ap = ap.rearrange("p (a b c) -> p a b c", b=shape[1], c=shape[2])
        return ap, ("arena", off * 4, (off + n) * 4)

    def bf16(self, off, shape, p=None):
        n = int(np.prod(shape))
        w = (n + 1) // 2
        assert off + w <= self.words, (off, w, self.words)
        ps = slice(None) if p is None else slice(p[0], p[1])
        ap = self.h[ps, off:off + w].bitcast(BF16)[:, 0:n]
        if len(shape) == 2:
            ap = ap.rearrange("p (a b) -> p a b", b=shape[1])
        return ap, ("arena", off * 4, (off + w) * 4)


class Cfg:
    def __init__(self, depth=4, passes=((1024, True), (1024, False)), out_prompt=True, debug=0):
        self.debug = debug
        self.depth = depth
        self.passes = passes
        self.seq = sum(p[0] for p in passes)
        self.tmax = max(p[0] + (NS if p[1] else 0) for p in passes)


def build_program(cfg):
    nc = bass.Bass("TRN2", target_bir_lowering=False)
    L = cfg.depth
    SEQ = cfg.seq
    TMAX = cfg.tmax

    def din(name, shape):
        return nc.dram_tensor(name, list(shape), F32, kind="ExternalInput").ap()

    def dout(name, shape):
        return nc.dram_tensor(name, list(shape), F32, kind="ExternalOutput").ap()

    xp = din("xp", (SEQ, D))
    xs = din("xs", (NS, D))
    cin = din("c", (1 + NSS, D))
    sd = din("sd", (L, NSS, H, DK, DV))
    sdc = din("sdc", (L, NSS * 3, NDC))
    sl = din("sl", (L, NSS, WB))
    slc = din("slc", (L, NSS * 3, WB))
    w_in = din("w_in", (L, D, N_IN))
    w_out = din("w_out", (L, D, D))
    norm_g = din("norm_g", (L, 3, D))
    w_ada = din("w_ada", (L, D, 9 * D) if not cfg.debug else (1, 128, 128))
    b_ada = din("b_ada", (L, 9 * D))
    fshape = (L, 2, D, DFF) if not cfg.debug else (1, 1, 128, 128)
    ffn_w1 = din("ffn_w1", fshape)
    ffn_w3 = din("ffn_w3", fshape)
    ffn_w2 = din("ffn_w2", (L, 2, DFF, D) if not cfg.debug else (1, 1, 128, 128))
    dconv_w = din("dconv_w", (L, 4, NDC))
    d_alog = din("d_alog", (L, H))
    d_dtbias = din("d_dtbias", (L, H))
    d_onorm = din("d_onorm", (L, DV))
    lconv_w = din("lconv_w", (L, 4, WB))
    lconv_b = din("lconv_b", (L, WB))
    lru_wa = din("lru_wa", (L, NB, 128, 128))
    lru_ba = din("lru_ba", (L, WB))
    lru_wx = din("lru_wx", (L, NB, 128, 128))
    lru_bx = din("lru_bx", (L, WB))
    lru_lam = din("lru_lam", (L, WB))
    final_g = din("final_g", (D,))
    cst = din("cst", (128, 8, 128))
    seqrow = din("seqrow", (NSS * NS,))
    seqcol = din("seqcol", (NS, NSS))

    yp = dout("yp", (SEQ, D))
    ys = dout("ys", (NS, D))
    pd = dout("pd", (L, H, DK, DV))
    pdc = dout("pdc", (L, 3, NDC))
    pl = dout("pl", (L, WB))
    plc = dout("plc", (L, 3, WB))
    sd_o = dout("sd_o", (L, NSS, H, DK, DV))
    sdc_o = dout("sdc_o", (L, NSS, 3, NDC))
    sl_o = dout("sl_o", (L, NSS, WB))
    slc_o = dout("slc_o", (L, NSS, 3, WB))
    sscr = nc.dram_tensor("sscr", [L, H, DK, DV], F32, kind="Internal").ap()

    P = Prog()
    es = ExitStack()
    with es:
        def sbuf(name, shape, dt=F32):
            h = es.enter_context(nc.sbuf_tensor(name, list(shape), dt))
            return TT(h, name, shape, 4 if dt == F32 else 2)

        xT = sbuf("xT", (128, KC, TMAX))
        hT = sbuf("hT", (128, KC, TMAX), BF16)
        ring = sbuf("ring", (128, NSLOT, SLOT), BF16)
        CST = sbuf("CST", (128, 8, 128))
        onesb = sbuf("onesb", (128, 128), BF16)
        seqrowT = sbuf("seqrowT", (128, NSS, NS))
        seqcolT = sbuf("seqcolT", (NS, NSS))
        modT = sbuf("modT", (128, 144, 1 + NSS))
        modsave = sbuf("modsave", (128, L, 144))
        csT = sbuf("csT", (128, KC, 1 + NSS), BF16)
        ngT = sbuf("ngT", (128, 3, KC))
        badaT = sbuf("badaT", (128, 144))
        fgT = sbuf("fgT", (128, KC))
        dcwT = sbuf("dcwT", (128, 24, 4))
        lcwT = sbuf("lcwT", (128, 8, 4))
        lprm = sbuf("lprm", (128, 5, 8))
        onT = sbuf("onT", (128, 1))
        rowp = sbuf("rowp", (128, 2, 8))
        epsc = sbuf("epsc", (128, 2))
        carry_c = sbuf("carry_c", (128, L, 32, 3))
        carry_h = sbuf("carry_h", (128, L, 8))
        AW = 8448
        arena_t = sbuf("arena", (128, AW))
        AR = Arena(arena_t.h, AW)
        PS = []
        for i in range(8):
            h = es.enter_context(nc.psum_tensor(f"ps{i}", [128, 512], F32))
            PS.append(TT(h, f"ps{i}", (128, 512), 4))
        esem = {e: es.enter_context(nc.semaphore("sem_" + e)) for e in ENGS}

        def newslot(name):
            return Slot(es.enter_context(nc.semaphore(name)))

        ring_slots = [newslot(f"ring{i}") for i in range(NSLOT)]
        ld_slots = [newslot(f"ld{i}") for i in range(8)]
        st_slots = [newslot(f"st{i}") for i in range(8)]
        st_i = [0]
        ld_i = [0]

        def ldslot():
            ld_i[0] += 1
            return ld_slots[ld_i[0] % len(ld_slots)]

        def stslot():
            st_i[0] += 1
            return st_slots[st_i[0] % len(st_slots)]

        IDENT = CST.v(0)
        ONESF = CST.v(1)
        MLE = CST.v(2)
        ML = CST.v(3)
        BONES = CST.v(4)
        MLE4 = CST.v(5)
        ML4 = CST.v(6)
        BONES4 = CST.v(7)
        NOREG = ()

        def act(out, in_, func, reads, writes, bias=None, scale=None):
            kw = {}
            if bias is not None:
                kw["bias"] = bias
            if scale is not None:
                kw["scale"] = scale
            P.add("act", I("activation", out=out, in_=in_, func=func, **kw), reads, writes)

        def dve(fn, reads, writes):
            P.add("dve", fn, reads, writes)

        ring_n = [0]

        def wload(parts):
            s = ring_n[0] % NSLOT
            ring_n[0] += 1
            for (dst_ap, src_ap) in [pp(s) for pp in parts]:
                P.dma("pool", dst_ap, src_ap, ring_slots[s], writes=[ring.v(s)[1]])
            return s

        def wload_col(wmat, c0, ncol):
            src = wmat.rearrange("(k p) n -> p k n", p=128)[:, :, c0:c0 + ncol]

            def part(s):
                dst = ring.h[:, s, 0:KC * ncol].rearrange("p (k n) -> p k n", n=ncol)
                return dst, src
            s = wload([part])
            reg = ring.v(s)[1]

            my = ring_n[0]

            def get(kc, j0, j1):
                assert ring_n[0] - my < NSLOT, "ring slot reused while still referenced"
                return ring.h[:, s, kc * ncol + j0:kc * ncol + j1]
            return get, reg

        def wload_row(wmat, r0, nrow):
            na = nrow // 128
            src = wmat[r0:r0 + nrow, :].rearrange("(a p) n -> p a n", p=128)

            def part(s):
                dst = ring.h[:, s, 0:na * D].rearrange("p (a n) -> p a n", n=D)
                return dst, src
            s = wload([part])
            reg = ring.v(s)[1]

            my = ring_n[0]

            def get(a, j0, j1):
                assert ring_n[0] - my < NSLOT, "ring slot reused while still referenced"
                return ring.h[:, s, a * D + j0:a * D + j1]
            return get, reg

        ps_i = [0]

        def dbank():
            ps_i[0] += 1
            return PS[ps_i[0] % 4]

        sps_i = [0]

        def sbank():
            sps_i[0] += 1
            k = sps_i[0] % 4
            return PS[4 + k], 0

        def mm(out, lhsT, rhs, start=True, stop=True):
            return I("matmul", out, lhsT=lhsT, rhs=rhs, start=start, stop=stop)

        def tr(out, in_, ident):
            return I("transpose", out, in_, ident)

        def setup():
            sl0 = ldslot()
            P.dma("sp", CST.h[:], cst, sl0, writes=[CST.v()[1]])
            P.dma("sp", seqrowT.h[:].rearrange("p a b -> p (a b)"), seqrow.partition_broadcast(128), sl0,
                  writes=[seqrowT.v()[1]])
            P.dma("sp", seqcolT.h[:], seqcol, sl0, writes=[seqcolT.v()[1]])
            P.dma("sp", fgT.h[:], final_g.rearrange("(k p) -> p k", p=128), sl0, writes=[fgT.v()[1]], slow=True)
            dve(I("memset", onesb.h[:], 1.0), [], [onesb.v()[1]])
            dve(I("memset", epsc.h[:, 0:1], EPS), [], [epsc.v()[1]])
            dve(I("memset", epsc.h[:, 1:2], 1.0), [], [epsc.v()[1]])
            dve(I("memset", carry_c.h[:], 0.0), [], [carry_c.v()[1]])
            dve(I("memset", carry_h.h[:], 0.0), [], [carry_h.v()[1]])
            ctok, ctr = AR.f32(0, (D,), p=(0, 1 + NSS))
            P.dma("sp", ctok, cin, sl0, writes=[ctr])
            act(ctok, ctok, AF.Silu, [ctr], [ctr])
            for g4 in range(4):
                pt = dbank()
                fns = []
                for j in range(4):
                    kc = g4 * 4 + j
                    fns.append(tr(pt.h[:, j * 32:j * 32 + 1 + NSS], ctok[:, kc * 128:(kc + 1) * 128],
                                  CST.h[0:1 + NSS, 0, 0:1 + NSS]))
                P.add("pe", fns, [ctr, IDENT[1]], [pt.v()[1]])
                dve(I("tensor_copy",
                    out=csT.h[:, g4 * 4:g4 * 4 + 4, :],
                    in_=pt.h[:, 0:128].rearrange("p (a b) -> p a b", b=32)[:, :, 0:1 + NSS]),
                    [pt.v()[1]], [csT.v()[1]])

        def layer_params(l):
            s0 = ldslot()
            with nc.allow_non_contiguous_dma(reason="small transposed parameter loads"):
                for s_ in range(3):
                    P.dma("sp", ngT.h[:, s_, :], norm_g[l, s_].rearrange("(k p) -> p k", p=128), s0, writes=[ngT.v()[1]], slow=True)
                P.dma("sp", badaT.h[:], b_ada[l].rearrange("(q p) -> p q", p=128), s0, writes=[badaT.v()[1]], slow=True)
                for j_ in range(4):
                    P.dma("sp", dcwT.h[:, :, j_], dconv_w[l, j_].rearrange("(b p) -> p b", p=128), s0, writes=[dcwT.v()[1]], slow=True)
                    P.dma("sp", lcwT.h[:, :, j_], lconv_w[l, j_].rearrange("(b p) -> p b", p=128), s0, writes=[lcwT.v()[1]], slow=True)
                P.dma("sp", lprm.h[:, 0, :], lconv_b[l].rearrange("(b p) -> p b", p=128), s0, writes=[lprm.v()[1]], slow=True)
                P.dma("sp", lprm.h[:, 1, :], lru_ba[l].rearrange("(b p) -> p b", p=128), s0, writes=[lprm.v()[1]], slow=True)
                P.dma("sp", lprm.h[:, 2, :], lru_bx[l].rearrange("(b p) -> p b", p=128), s0, writes=[lprm.v()[1]], slow=True)
                P.dma("sp", lprm.h[:, 3, :], lru_lam[l].rearrange("(b p) -> p b", p=128), s0, writes=[lprm.v()[1]], slow=True)
                P.dma("sp", onT.h[:], d_onorm[l].rearrange("(p o) -> p o", o=1), s0, writes=[onT.v()[1]])
                P.dma("sp", rowp.h[:, 0, :], d_dtbias[l].partition_broadcast(128), s0, writes=[rowp.v()[1]])
                P.dma("sp", rowp.h[:, 1, :], d_alog[l].partition_broadcast(128), s0, writes=[rowp.v()[1]])
            act(rowp.h[:, 1, :], rowp.h[:, 1, :], AF.Exp, [rowp.v()[1]], [rowp.v()[1]])
            dve(I("tensor_scalar", out=rowp.h[:, 1, :], in0=rowp.h[:, 1, :], scalar1=-1.0, scalar2=None,
                                          op0=ALU.mult), [rowp.v()[1]], [rowp.v()[1]])
            act(lprm.h[:, 3, :], lprm.h[:, 3, :], AF.Exp, [lprm.v()[1]], [lprm.v()[1]], scale=-1.0)
            act(lprm.h[:, 3, :], lprm.h[:, 3, :], AF.Ln, [lprm.v()[1]], [lprm.v()[1]], bias=epsc.h[:, 1:2])
            dve(I("tensor_scalar", out=lprm.h[:, 3, :], in0=lprm.h[:, 3, :], scalar1=-8.0, scalar2=None,
                                          op0=ALU.mult), [lprm.v()[1]], [lprm.v()[1]])

        def ada(l):
            QB = 24
            pt = None
            for ct in range(72):
                get, wreg = wload_col(w_ada[l], ct * 256, 256)
                for j in range(2):
                    q = ct * 2 + j
                    if q % QB == 0:
                        pt = dbank()
                    o = (q % QB) * (1 + NSS)
                    fns = [mm(pt.h[:, o:o + 1 + NSS], get(kc, j * 128, (j + 1) * 128), csT.h[:, kc, :],
                              start=(kc == 0), stop=(kc == KC - 1)) for kc in range(KC)]
                    P.add("pe", fns, [wreg, csT.v()[1]], [pt.v()[1]])
                    if q % QB == QB - 1:
                        q0 = q - QB + 1
                        dve(I("tensor_tensor",
                            out=modT.h[:, q0:q0 + QB, :],
                            in0=pt.h[:, 0:QB * (1 + NSS)].rearrange("p (a b) -> p a b", b=1 + NSS),
                            in1=badaT.h[:, q0:q0 + QB].unsqueeze(2).to_broadcast([128, QB, 1 + NSS]), op=ALU.add),
                            [pt.v()[1], badaT.v()[1]], [modT.v()[1]])
            mreg = modT.v()[1]
            for sub in range(3):
                qs = sub * 48 + 16
                dve(I("tensor_scalar", out=modT.h[:, qs:qs + 16, :], in0=modT.h[:, qs:qs + 16, :],
                                                     scalar1=1.0, scalar2=None, op0=ALU.add), [mreg], [mreg])
                dve(I("tensor_tensor",
                    out=modT.h[:, qs:qs + 16, :], in0=modT.h[:, qs:qs + 16, :],
                    in1=ngT.h[:, sub, :].unsqueeze(2).to_broadcast([128, 16, 1 + NSS]), op=ALU.mult),
                    [mreg, ngT.v()[1]], [mreg])
                if sub != 1:
                    qg = sub * 48 + 32
                    dve(I("tensor_scalar", out=modT.h[:, qg:qg + 16, :], in0=modT.h[:, qg:qg + 16, :],
                                                         scalar1=0.5, scalar2=None, op0=ALU.mult), [mreg], [mreg])
            dve(I("tensor_copy", out=modsave.h[:, l, :], in_=modT.h[:, :, 0]), [mreg], [modsave.v()[1]])

        def ada_restore(l):
            dve(I("tensor_copy", out=modT.h[:, :, 0], in_=modsave.h[:, l, :]), [modsave.v()[1]],
                [modT.v()[1]])

        class PassCtx:
            pass

        def run_pass(pi):
            npr, has_s = cfg.passes[pi]
            t0 = sum(p[0] for p in cfg.passes[:pi])
            last = pi == len(cfg.passes) - 1
            first = pi == 0
            T = npr + (NS if has_s else 0)
            dtiles = [(c, min(c + 512, npr)) for c in range(0, npr, 512)]
            if has_s:
                dtiles_all = dtiles + [(npr, T)]
            else:
                dtiles_all = list(dtiles)
            mtiles = [(c, c + 128) for c in range(0, npr, 128)]
            assert npr % 128 == 0
            SC0 = npr

            xreg = lambda kc, c0, c1: xT.v(kc, (c0, c1))[1]
            hreg = lambda kc, c0, c1: hT.v(kc, (c0, c1))[1]

            def load_x():
                tiles = [(xp, t0 + c0, c0, 128) for (c0, c1) in mtiles]
                if has_s:
                    tiles.append((xs, 0, SC0, NS))
                for i, (src, r0, c0, n) in enumerate(tiles):
                    stg, sreg = AR.f32((i % 2) * D, (D,), p=(0, n))
                    P.dma("sp", stg, src[r0:r0 + n, :], ldslot(), writes=[sreg])
                    for g4 in range(4):
                        pt = dbank()
                        fns = [tr(pt.h[:, j * 128:j * 128 + n], stg[:, (g4 * 4 + j) * 128:(g4 * 4 + j + 1) * 128],
                                  CST.h[0:n, 0, 0:n]) for j in range(4)]
                        P.add("pe", fns, [sreg], [pt.v()[1]])
                        o_ap = xT.h[:, g4 * 4:g4 * 4 + 4, c0:c0 + n]
                        i_ap = pt.h[:, :].rearrange("p (a b) -> p a b", b=128)[:, :, 0:n]
                        wr = [xreg(g4 * 4 + j, c0, c0 + n) for j in range(4)]
                        if g4 % 2 == 0:
                            dve(I("tensor_copy", out=o_ap, in_=i_ap), [pt.v()[1]], wr)
                        else:
                            P.add("act", I("copy", out=o_ap, in_=i_ap), [pt.v()[1]], wr)

            NRM0 = AW - 2560

            def rstd_tile(c0, c1, k):
                n = c1 - c0
                pt = dbank()
                for kc in range(KC):
                    sq, sqr = AR.bf16(NRM0 + (kc % 2) * 256, (n,))
                    act(sq, xT.h[:, kc, c0:c1], AF.Square, [xreg(kc, c0, c1)], [sqr])
                    P.add("pe", mm(pt.h[:, 0:n], onesb.h[:], sq, start=(kc == 0), stop=(kc == KC - 1)),
                          [sqr], [pt.v((0, n))[1]])
                rs, rsr = AR.f32(NRM0 + 512 + (k % 2) * 512, (n,))
                act(rs, pt.h[:, 0:n], AF.Sqrt, [pt.v((0, n))[1]], [rsr], bias=epsc.h[:, 0:1], scale=1.0 / D)
                dve(I("reciprocal", out=rs, in_=rs), [rsr], [rsr])
                return rs, rsr

            def norm_mod(sub):
                qsh, qsc = sub * 48, sub * 48 + 16
                for k, (c0, c1) in enumerate(dtiles_all):
                    n = c1 - c0
                    rs, rsr = rstd_tile(c0, c1, k)
                    for kc in range(KC):
                        tmp, tr_ = AR.f32(NRM0 + 1536 + (kc % 2) * 512, (n,))
                        dve(I("tensor_tensor", out=tmp, in0=xT.h[:, kc, c0:c1], in1=rs,
                                                                      op=ALU.mult),
                            [xreg(kc, c0, c1), rsr], [tr_])
                        if c0 < npr:
                            act(hT.h[:, kc, c0:c1], tmp, AF.Identity, [tr_, modT.v()[1]], [hreg(kc, c0, c1)],
                                bias=modT.h[:, qsh + kc, 0:1], scale=modT.h[:, qsc + kc, 0:1])
                        else:
                            t3 = tmp.rearrange("p (s t) -> p s t", t=LS)
                            dve(I("tensor_tensor",
                                out=t3, in0=t3, in1=modT.h[:, qsc + kc, 1:1 + NSS].unsqueeze(2).to_broadcast(
                                    [128, NSS, LS]), op=ALU.mult), [tr_, modT.v()[1]], [tr_])
                            dve(I("tensor_tensor",
                                out=hT.h[:, kc, c0:c1].rearrange("p (s t) -> p s t", t=LS), in0=t3,
                                in1=modT.h[:, qsh + kc, 1:1 + NSS].unsqueeze(2).to_broadcast([128, NSS, LS]),
                                op=ALU.add), [tr_, modT.v()[1]], [hreg(kc, c0, c1)])

            def x_update(pt, n, dm, c0, c1, qg):
                wr = [xreg(dm, c0, c1)]
                if c0 < npr:
                    dve(I("scalar_tensor_tensor", out=xT.h[:, dm, c0:c1], in0=pt.h[:, 0:n],
                                                         scalar=modT.h[:, qg + dm, 0:1], in1=xT.h[:, dm, c0:c1],
                                                         op0=ALU.mult, op1=ALU.add),
                        [pt.v((0, n))[1], modT.v()[1]] + wr, wr)
                else:
                    tmp, tr_ = AR.f32(NRM0 + 1536 + (dm % 2) * 512, (n,))
                    dve(I("tensor_tensor",
                        out=tmp.rearrange("p (s t) -> p s t", t=LS),
                        in0=pt.h[:, 0:n].rearrange("p (s t) -> p s t", t=LS),
                        in1=modT.h[:, qg + dm, 1:1 + NSS].unsqueeze(2).to_broadcast([128, NSS, LS]), op=ALU.mult),
                        [pt.v((0, n))[1], modT.v()[1]], [tr_])
                    dve(I("tensor_tensor", out=xT.h[:, dm, c0:c1], in0=xT.h[:, dm, c0:c1], in1=tmp,
                                                  op=ALU.add), [tr_] + wr, wr)

            def ffn(l, j, sub):
                norm_mod(sub)
                qg = sub * 48 + 32
                G = 8
                HID0 = 0
                SIL0 = (G * TMAX + 1) // 2 + 8
                assert SIL0 + 1024 <= NRM0
                for f0 in range(0, FCN, G):
                    f1 = min(FCN, f0 + G)
                    for fp in range(f0, f1, 2):
                        nf = min(2, f1 - fp)
                        g1, r1 = wload_col(ffn_w1[l, j], fp * 128, nf * 128)
                        g3, r3 = wload_col(ffn_w3[l, j], fp * 128, nf * 128)
                        for fi in range(nf):
                            fc = fp + fi
                            for k, (c0, c1) in enumerate(dtiles_all):
                                n = c1 - c0
                                p1, p3 = dbank(), dbank()
                                hr = [hreg(kc, c0, c1) for kc in range(KC)]
                                P.add("pe", [mm(p1.h[:, 0:n], g1(kc, fi * 128, fi * 128 + 128), hT.h[:, kc, c0:c1],
                                                start=(kc == 0), stop=(kc == KC - 1)) for kc in range(KC)],
                                      [r1] + hr, [p1.v((0, n))[1]])
                                P.add("pe", [mm(p3.h[:, 0:n], g3(kc, fi * 128, fi * 128 + 128), hT.h[:, kc, c0:c1],
                                                start=(kc == 0), stop=(kc == KC - 1)) for kc in range(KC)],
                                      [r3] + hr, [p3.v((0, n))[1]])
                                st_, str_ = AR.f32(SIL0 + ((fc + k) % 2) * 512, (n,))
                                act(st_, p1.h[:, 0:n], AF.Silu, [p1.v((0, n))[1]], [str_])
                                hid, hidr = AR.bf16(HID0 + ((fc - f0) * TMAX) // 2 + c0 // 2, (n,))
                                dve(I("tensor_tensor",
                                    out=hid, in0=st_, in1=p3.h[:, 0:n], op=ALU.mult),
                                    [str_, p3.v((0, n))[1]], [hidr])
                    w2 = []
                    for fp in range(f0, f1, 2):
                        nf = min(2, f1 - fp)
                        w2.append((fp, nf) + wload_row(ffn_w2[l, j], fp * 128, nf * 128))
                    for dm in range(KC):
                        for (c0, c1) in dtiles_all:
                            n = c1 - c0
                            pt = dbank()
                            fns, rds = [], []
                            cnt = f1 - f0
                            i = 0
                            for (fp, nf, g2, r2) in w2:
                                rds.append(r2)
                                for fi in range(nf):
                                    fc = fp + fi
                                    hid, hidr = AR.bf16(HID0 + ((fc - f0) * TMAX) // 2 + c0 // 2, (n,))
                                    rds.append(hidr)
                                    fns.append(mm(pt.h[:, 0:n], g2(fi, dm * 128, dm * 128 + 128), hid,
                                                  start=(i == 0), stop=(i == cnt - 1)))
                                    i += 1
                            P.add("pe", fns, rds, [pt.v((0, n))[1]])
                            x_update(pt, n, dm, c0, c1, qg)

            def final_out():
                tiles = [(yp, t0 + c0, c0, 128) for (c0, c1) in mtiles]
                if has_s:
                    tiles.append((ys, 0, SC0, NS))
                rst = {}
                for k, (c0, c1) in enumerate(dtiles_all):
                    rst[k] = rstd_tile(c0, c1, k) + (c0, c1)
                    rs, rsr, _, _ = rst[k]
                    for i, (dst, r0, cc0, n) in enumerate(tiles):
                        if not (c0 <= cc0 < c1):
                            continue
                        stg, sreg = AR.f32((i % 2) * D, (D,), p=(0, n))
                        for g4 in range(4):
                            pt = dbank()
                            for jj in range(4):
                                kc = g4 * 4 + jj
                                yt, ytr = AR.f32(2 * D + (kc % 4) * 128, (n,))
                                dve(I("scalar_tensor_tensor",
                                    out=yt, in0=xT.h[:, kc, cc0:cc0 + n], scalar=fgT.h[:, kc:kc + 1],
                                    in1=rs[:, cc0 - c0:cc0 - c0 + n], op0=ALU.mult, op1=ALU.mult),
                                    [xreg(kc, cc0, cc0 + n), rsr], [ytr])
                                P.add("pe", tr(pt.h[0:n, jj * 128:(jj + 1) * 128], yt, IDENT[0]), [ytr],
                                      [pt.v((jj * 128, jj * 128 + 128))[1]])
                            P.add("act", I("copy",
                                out=stg[:, g4 * 512:(g4 + 1) * 512], in_=pt.h[0:n, :]), [pt.v()[1]], [sreg])
                        P.dma("sp", dst[r0:r0 + n, :], stg, stslot(), reads=[sreg], final=True)

            load_x()
            for l in range(L):
                layer_params(l)
                if cfg.debug:
                    dve(I("memset", modT.h[:], 0.25), [], [modT.v()[1]])
                elif first:
                    ada(l)
                else:
                    ada_restore(l)
                if not cfg.debug:
                    ffn(l, 0, 0)
                mixer(l, PassCtx, pi, npr, has_s, t0, T, dtiles, dtiles_all, mtiles, SC0, first, last, norm_mod,
                      x_update, xreg, hreg)
                if not cfg.debug:
                    ffn(l, 1, 2)
            final_out()

        def mixer(l, ctx, pi, npr, has_s, t0, T, dtiles, dtiles_all, mtiles, SC0, first, last, norm_mod, x_update,
                  xreg, hreg):
            norm_mod(1)
            QG = 48 + 32
            RAW, QS, KS, VS, GS, SQ = 0, 520, 1032, 1544, 2056, 2568
            SST, OTP, GST = 3080, 3208, 3208 + TMAX
            CTX = GST + 448
            XTR = CTX + 17 * 128
            assert XTR + 1024 <= AW, (XTR, AW)
            ntile_p = len(mtiles)

            def Cm(i, rows, cols, r0=0):
                return AR.f32(CTX + i * 128, (cols,), p=(r0, r0 + rows))

            def V(off, n, c0=0, p=None):
                return AR.f32(off + c0, (n,), p=p)

            otp_ap = AR.h[:, OTP:OTP + TMAX].bitcast(BF16)
            otp_reg = ("arena", OTP * 4, (OTP + TMAX) * 4)

            def gst(ti, q, n):
                off = GST + ti * 48 + q * 8
                return AR.f32(off, (8,), p=(0, n))

            gab, rab = wload_col(w_in[l], OFF_A, 16)
            alltiles = [(ti, c0, c1, 128, MLE, BONES) for ti, (c0, c1) in enumerate(mtiles)]
            if has_s:
                alltiles.append((ntile_p, SC0, SC0 + NS, NS, MLE4, BONES4))
            Zt = AR.f32(GST + 432, (8,))
            NZt = AR.f32(GST + 440, (8,))
            for (ti, c0, c1, n, mle, bon) in alltiles:
                pt, pc = sbank()
                pab = pt.h[0:n, pc:pc + 16]
                pabr = pt.v((pc, pc + 16))[1]
                P.add("pe", [mm(pab, hT.h[:, kc, c0:c1], gab(kc, 0, 16), start=(kc == 0), stop=(kc == KC - 1))
                             for kc in range(KC)], [rab] + [hreg(kc, c0, c1) for kc in range(KC)], [pabr])
                z, zr = Zt[0][0:n], Zt[1]
                nz, nzr = NZt[0][0:n], NZt[1]
                g_, gr = gst(ti, 0, n)
                nb_, nbr = gst(ti, 1, n)
                gc_, gcr = gst(ti, 2, n)
                bg_, bgr = gst(ti, 3, n)
                kd_, kdr = gst(ti, 4, n)
                be_, ber = gst(ti, 5, n)
                dve(I("tensor_tensor", out=z, in0=pt.h[0:n, pc:pc + 8], in1=rowp.h[0:n, 0, :], op=ALU.add),
                    [pabr, rowp.v()[1]], [zr])
                dve(I("tensor_scalar", out=nz, in0=z, scalar1=-1.0, scalar2=None, op0=ALU.mult), [zr], [nzr])
                dve(I("tensor_tensor", out=nz, in0=nz, in1=z, op=ALU.max), [zr, nzr], [nzr])
                act(nz, nz, AF.Exp, [nzr], [nzr], scale=-1.0)
                act(nz, nz, AF.Ln, [nzr], [nzr], bias=epsc.h[0:n, 1:2])
                dve(I("scalar_tensor_tensor", out=z, in0=z, scalar=0.0, in1=nz, op0=ALU.max, op1=ALU.add),
                    [zr, nzr], [zr])
                dve(I("tensor_tensor", out=g_, in0=z, in1=rowp.h[0:n, 1, :], op=ALU.mult), [zr, rowp.v()[1]], [gr])
                act(be_, pt.h[0:n, pc + 8:pc + 16], AF.Sigmoid, [pabr], [ber])
                dve(I("tensor_scalar", out=nb_, in0=be_, scalar1=-1.0, scalar2=None, op0=ALU.mult), [ber], [nbr])
                p2, pc2 = sbank()
                pgc = p2.h[0:n, pc2:pc2 + 8]
                pgl = p2.h[0:n, pc2 + 8:pc2 + 16]
                pgr = p2.v((pc2, pc2 + 16))[1]
                P.add("pe", [mm(pgc, mle[0][0:n, 0:n], g_), mm(pgl, bon[0][0:n, 0:n], g_)], [gr], [pgr])
                dve(I("tensor_copy", out=gc_, in_=pgc), [pgr], [gcr])
                act(z, pgc, AF.Exp, [pgr], [zr])
                dve(I("tensor_tensor", out=bg_, in0=be_, in1=z, op=ALU.mult), [zr, ber], [bgr])
                dve(I("tensor_tensor", out=nz, in0=pgl, in1=gc_, op=ALU.subtract), [pgr, gcr], [nzr])
                act(kd_, nz, AF.Exp, [nzr], [kdr])

            def conv_piece(ps_t, n, blk, taps, out_off, sample, bias=None, st_src=None, st_col0=0):
                psr = ps_t.v((0, n))[1]
                if not sample:
                    rawf = V(RAW, 3 + n)
                    cc = carry_c.v(l, blk)
                    dve(I("tensor_copy", out=AR.h[:, RAW:RAW + 3], in_=carry_c.h[:, l, blk, :]), [cc[1]], [rawf[1]])
                    P.add("act", I("copy", out=AR.h[:, RAW + 3:RAW + 3 + n], in_=ps_t.h[:, 0:n]), [psr], [rawf[1]])
                    src = lambda j: AR.h[:, RAW + j:RAW + j + n]
                    outv = AR.h[:, out_off:out_off + n]
                else:
                    rawf = V(RAW, NSS * 7)
                    r3 = AR.h[:, RAW:RAW + NSS * 7].rearrange("p (s j) -> p s j", j=7)
                    stg, stgr = AR.f32(XTR, (128,), p=(0, NSS * 3))
                    P.dma("sp", stg, st_src[l][:, st_col0:st_col0 + 128], ldslot(), writes=[stgr])
                    p2, pc2 = sbank()
                    P.add("pe", tr(p2.h[:, pc2:pc2 + NSS * 3], stg, CST.h[0:NSS * 3, 0, 0:NSS * 3]), [stgr],
                          [p2.v((pc2, pc2 + 128))[1]])
                    dve(I("tensor_copy", out=r3[:, :, 0:3],
                          in_=p2.h[:, pc2:pc2 + NSS * 3].rearrange("p (s j) -> p s j", j=3)),
                        [p2.v((pc2, pc2 + 128))[1]], [rawf[1]])
                    P.add("act", I("copy", out=r3[:, :, 3:7],
                                   in_=ps_t.h[:, 0:n].rearrange("p (s t) -> p s t", t=LS)), [psr], [rawf[1]])
                    src = lambda j: r3[:, :, j:j + LS]
                    outv = AR.h[:, out_off:out_off + n].rearrange("p (s t) -> p s t", t=LS)
                outr = ("arena", out_off * 4, (out_off + n) * 4)
                treg = taps[1]
                tp = taps[0]
                if bias is None:
                    dve(I("tensor_scalar", out=outv, in0=src(0), scalar1=tp[:, 0:1], scalar2=None, op0=ALU.mult),
                        [rawf[1], treg], [outr])
                else:
                    dve(I("tensor_scalar", out=outv, in0=src(0), scalar1=tp[:, 0:1], scalar2=bias[0], op0=ALU.mult,
                          op1=ALU.add), [rawf[1], treg, bias[1]], [outr])
                for j in range(1, 4):
                    dve(I("scalar_tensor_tensor", out=outv, in0=src(j), scalar=tp[:, j:j + 1], in1=outv,
                          op0=ALU.mult, op1=ALU.add), [rawf[1], treg, outr], [outr])
                if not sample:
                    dve(I("tensor_copy", out=carry_c.h[:, l, blk, :], in_=AR.h[:, RAW + n:RAW + n + 3]), [rawf[1]],
                        [carry_c.v(l, blk)[1]])
                return outr

            def proj_piece(get, wreg, j0, c0, c1):
                n = c1 - c0
                pt = dbank()
                P.add("pe", [mm(pt.h[:, 0:n], get(kc, j0, j0 + 128), hT.h[:, kc, c0:c1], start=(kc == 0),
                                stop=(kc == KC - 1)) for kc in range(KC)],
                      [wreg] + [hreg(kc, c0, c1) for kc in range(KC)], [pt.v((0, n))[1]])
                return pt

            def conv_state_out(get, wreg, ncols, dst_p, dst_s, col0):
                jobs = []
                if last:
                    jobs.append(("p", npr - 4, npr, 4))
                if has_s:
                    jobs.append(("s", SC0, SC0 + NS, NS))
                for (kind, c0, c1, m) in jobs:
                    pt = dbank()
                    P.add("pe", [mm(pt.h[0:m, 0:ncols], hT.h[:, kc, c0:c1], get(kc, 0, ncols), start=(kc == 0),
                                    stop=(kc == KC - 1)) for kc in range(KC)],
                          [wreg] + [hreg(kc, c0, c1) for kc in range(KC)], [pt.v((0, ncols))[1]])
                    stg, stgr = AR.f32(XTR + 256, (ncols,), p=(0, m))
                    dve(I("tensor_copy", out=stg, in_=pt.h[0:m, 0:ncols]), [pt.v((0, ncols))[1]], [stgr])
                    if kind == "p":
                        P.dma("sp", dst_p[l, :, col0:col0 + ncols], AR.h[1:4, XTR + 256:XTR + 256 + ncols], stslot(),
                              reads=[stgr], final=True)
                    else:
                        for t in range(1, LS):
                            P.dma("sp", dst_s[l, :, t - 1, col0:col0 + ncols],
                                  AR.h[t:NS:LS, XTR + 256:XTR + 256 + ncols], stslot(), reads=[stgr], final=True)

            def l2norm(off, n, scale):
                xr = ("arena", off * 4, (off + n) * 4)
                sq, sqr = V(SQ, n)
                xv = AR.h[:, off:off + n]
                dve(I("tensor_tensor", out=sq, in0=xv, in1=xv, op=ALU.mult), [xr], [sqr])
                pt = dbank()
                P.add("pe", mm(pt.h[:, 0:n], ONESF[0], sq), [sqr], [pt.v((0, n))[1]])
                act(sq, pt.h[:, 0:n], AF.Sqrt, [pt.v((0, n))[1]], [sqr], bias=epsc.h[:, 0:1], scale=1.0)
                dve(I("reciprocal", out=sq, in_=sq), [sqr], [sqr])
                if scale is None:
                    dve(I("tensor_tensor", out=xv, in0=xv, in1=sq, op=ALU.mult), [xr, sqr], [xr])
                else:
                    dve(I("scalar_tensor_tensor", out=xv, in0=xv, scalar=scale, in1=sq, op0=ALU.mult, op1=ALU.mult),
                        [xr, sqr], [xr])

            def prep(h, ti, n, a0, mle, ml, par):
                nlev = 5 if n == 128 else 1
                R_ = {}
                ks = AR.h[:, KS + a0:KS + a0 + n]
                ksr = ("arena", (KS + a0) * 4, (KS + a0 + n) * 4)
                qs = AR.h[:, QS + a0:QS + a0 + n]
                qsr = ("arena", (QS + a0) * 4, (QS + a0 + n) * 4)
                vs = AR.h[:, VS + a0:VS + a0 + n]
                vsr = ("arena", (VS + a0) * 4, (VS + a0 + n) * 4)
                col = lambda q: (AR.h[0:n, GST + ti * 48 + q * 8 + h:GST + ti * 48 + q * 8 + h + 1], gst(ti, q, n)[1])
                g_, nb_, gc_, bg_, kd_, be_ = [col(q) for q in range(6)]
                B0 = 12
                kbg = Cm(0, n, 128)
                vb = Cm(1, n, 128)
                gm = Cm(2, n, 128)
                dm_ = Cm(3, n, n)
                dt_ = Cm(4, n, n)
                e1 = Cm(5, 128, n)
                mats = [Cm(6, n, n), Cm(7, n, n), Cm(9, n, n), Cm(10, n, n)]
                y_ = Cm(8, n, n)
                wT = Cm(B0 + 0, 128, n)
                u_ = Cm(B0 + 1, n, 128)
                qkT = Cm(B0 + 2, n, n)
                kdec = Cm(B0 + 3, n, 128)
                qdT = Cm(B0 + 4, 128, n)
                pk, pkc = sbank()
                pkr = pk.v((pkc, pkc + 128))[1]
                P.add("pe", tr(pk.h[0:n, pkc:pkc + 128], ks, IDENT[0]), [ksr], [pkr])
                pv, pvc = sbank()
                pvr = pv.v((pvc, pvc + 128))[1]
                P.add("pe", tr(pv.h[0:n, pvc:pvc + 128], vs, IDENT[0]), [vsr], [pvr])
                if cfg.debug == 511:
                    return R_
                dve(I("tensor_scalar", out=kbg[0], in0=pk.h[0:n, pkc:pkc + 128], scalar1=bg_[0], scalar2=None,
                      op0=ALU.mult), [pkr, bg_[1]], [kbg[1]])
                if cfg.debug == 512:
                    return R_
                dve(I("tensor_scalar", out=kdec[0], in0=pk.h[0:n, pkc:pkc + 128], scalar1=kd_[0], scalar2=None,
                      op0=ALU.mult), [pkr, kd_[1]], [kdec[1]])
                dve(I("tensor_scalar", out=vb[0], in0=pv.h[0:n, pvc:pvc + 128], scalar1=be_[0], scalar2=None,
                      op0=ALU.mult), [pvr, be_[1]], [vb[1]])
                if cfg.debug == 51:
                    return R_
                dve(I("tensor_scalar", out=gm[0], in0=ONESF[0][0:n, :], scalar1=g_[0], scalar2=None, op0=ALU.mult),
                    [g_[1]], [gm[1]])
                pg, pgc = sbank()
                pgr = pg.v((pgc, pgc + 128))[1]
                P.add("pe", mm(pg.h[:, pgc:pgc + n], gm[0], mle[0][0:n, 0:n]), [gm[1]], [pgr])
                dve(I("tensor_scalar", out=dm_[0], in0=pg.h[0:n, pgc:pgc + n], scalar1=gc_[0], scalar2=0.0,
                      op0=ALU.subtract, op1=ALU.max), [pgr, gc_[1]], [dm_[1]])
                act(dm_[0], dm_[0], AF.Exp, [dm_[1]], [dm_[1]], scale=-1.0)
                dve(I("tensor_tensor", out=dm_[0], in0=dm_[0], in1=ml[0][0:n, 0:n], op=ALU.mult), [dm_[1]], [dm_[1]])
                dve(I("tensor_scalar", out=dt_[0], in0=pg.h[0:n, pgc:pgc + n], scalar1=gc_[0], scalar2=0.0,
                      op0=ALU.subtract, op1=ALU.min), [pgr, gc_[1]], [dt_[1]])
                act(dt_[0], dt_[0], AF.Exp, [dt_[1]], [dt_[1]])
                dve(I("tensor_tensor", out=dt_[0], in0=dt_[0], in1=mle[0][0:n, 0:n], op=ALU.mult), [dt_[1]],
                    [dt_[1]])
                act(e1[0], pg.h[:, pgc:pgc + n], AF.Exp, [pgr], [e1[1]])
                dve(I("tensor_tensor", out=qdT[0], in0=qs, in1=e1[0], op=ALU.mult), [qsr, e1[1]], [qdT[1]])
                if n == 128:
                    egl = AR.f32(GST + 432 + 0, (2,)) if False else AR.f32(CTX + 11 * 128, (2,))
                    act(egl[0], pg.h[:, pgc + 63:pgc + 128:64], AF.Exp, [pgr], [egl[1]])
                else:
                    egl = AR.f32(CTX + 11 * 128, (NSS,))
                    act(egl[0], pg.h[:, pgc + LS - 1:pgc + n:LS], AF.Exp, [pgr], [egl[1]])
                R_["egl"] = egl
                if cfg.debug == 52:
                    return R_
                pkk, pkkc = sbank()
                pkkr = pkk.v((pkkc, pkkc + 128))[1]
                P.add("pe", mm(pkk.h[0:n, pkkc:pkkc + n], ks, ks), [ksr], [pkkr])
                pkq, pkqc = sbank()
                pkqr = pkq.v((pkqc, pkqc + 128))[1]
                P.add("pe", mm(pkq.h[0:n, pkqc:pkqc + n], ks, qs), [ksr, qsr], [pkqr])
                Bm, Am, A2m, B2m = mats
                dve(I("scalar_tensor_tensor", out=Bm[0], in0=pkk.h[0:n, pkkc:pkkc + n], scalar=nb_[0], in1=dm_[0],
                      op0=ALU.mult, op1=ALU.mult), [pkkr, nb_[1], dm_[1]], [Bm[1]])
                dve(I("tensor_tensor", out=qkT[0], in0=pkq.h[0:n, pkqc:pkqc + n], in1=dt_[0], op=ALU.mult),
                    [pkqr, dt_[1]], [qkT[1]])
                if cfg.debug == 53:
                    return R_
                pa, pac = sbank()
                par_ = pa.v((pac, pac + 128))[1]
                P.add("pe", tr(pa.h[0:n, pac:pac + n], Bm[0], IDENT[0][0:n, 0:n]), [Bm[1]], [par_])
                dve(I("tensor_copy", out=Am[0], in_=pa.h[0:n, pac:pac + n]), [par_], [Am[1]])
                dve(I("tensor_tensor", out=y_[0], in0=pa.h[0:n, pac:pac + n], in1=IDENT[0][0:n, 0:n], op=ALU.add),
                    [par_], [y_[1]])
                if cfg.debug == 54:
                    return R_
                curA, curB, nxtA, nxtB = Am, Bm, A2m, B2m
                for lev in range(nlev):
                    lastlev = lev == nlev - 1
                    pb, pbc = sbank()
                    pbr = pb.v((pbc, pbc + 128))[1]
                    P.add("pe", mm(pb.h[0:n, pbc:pbc + n], curA[0], curB[0]), [curA[1], curB[1]], [pbr])
                    dve(I("tensor_copy", out=nxtB[0], in_=pb.h[0:n, pbc:pbc + n]), [pbr], [nxtB[1]])
                    if not lastlev:
                        pa2, pa2c = sbank()
                        pa2r = pa2.v((pa2c, pa2c + 128))[1]
                        P.add("pe", mm(pa2.h[0:n, pa2c:pa2c + n], curB[0], curA[0]), [curA[1], curB[1]], [pa2r])
                        dve(I("tensor_copy", out=nxtA[0], in_=pa2.h[0:n, pa2c:pa2c + n]), [pa2r], [nxtA[1]])
                    py, pyc = sbank()
                    pyr = py.v((pyc, pyc + 128))[1]
                    P.add("pe", mm(py.h[0:n, pyc:pyc + n], nxtB[0], y_[0]), [nxtB[1], y_[1]], [pyr])
                    dve(I("tensor_tensor", out=y_[0], in0=py.h[0:n, pyc:pyc + n], in1=y_[0], op=ALU.add),
                        [pyr, y_[1]], [y_[1]])
                    curA, curB, nxtA, nxtB = nxtA, nxtB, curA, curB
                if cfg.debug == 55:
                    return R_
                pw, pwc = sbank()
                pwr = pw.v((pwc, pwc + 128))[1]
                P.add("pe", mm(pw.h[:, pwc:pwc + n], kbg[0], y_[0]), [kbg[1], y_[1]], [pwr])
                dve(I("tensor_copy", out=wT[0], in_=pw.h[:, pwc:pwc + n]), [pwr], [wT[1]])
                pu, puc = sbank()
                pur = pu.v((puc, puc + 128))[1]
                P.add("pe", mm(pu.h[0:n, puc:puc + 128], y_[0], vb[0]), [vb[1], y_[1]], [pur])
                dve(I("tensor_copy", out=u_[0], in_=pu.h[0:n, puc:puc + 128]), [pur], [u_[1]])
                R_.update(wT=wT, u=u_, qkT=qkT, kdec=kdec, qdT=qdT)
                return R_

            def o_finish(o_, n, a0, hh, ocol0):
                osq = Cm(5, n, 128)
                ss = AR.f32(CTX + 11 * 128 + 32, (1,), p=(0, n))
                dve(I("tensor_tensor", out=osq[0], in0=o_[0], in1=o_[0], op=ALU.mult), [o_[1]], [osq[1]])
                dve(I("reduce_sum", out=ss[0], in_=osq[0], axis=AX.X), [osq[1]], [ss[1]])
                act(ss[0], ss[0], AF.Sqrt, [ss[1]], [ss[1]], bias=epsc.h[0:n, 0:1], scale=1.0 / DV)
                dve(I("reciprocal", out=ss[0], in_=ss[0]), [ss[1]], [ss[1]])
                dve(I("tensor_scalar", out=osq[0], in0=o_[0], scalar1=ss[0], scalar2=None, op0=ALU.mult),
                    [o_[1], ss[1]], [osq[1]])
                po, poc = sbank()
                por = po.v((poc, poc + 128))[1]
                P.add("pe", tr(po.h[:, poc:poc + n], osq[0], IDENT[0][0:n, 0:n]), [osq[1]], [por])
                gsr = ("arena", (GS + a0) * 4, (GS + a0 + n) * 4)
                dve(I("scalar_tensor_tensor", out=otp_ap[:, hh * TMAX + ocol0:hh * TMAX + ocol0 + n],
                      in0=po.h[:, poc:poc + n], scalar=onT.h[:, 0:1], in1=AR.h[:, GS + a0:GS + a0 + n],
                      op0=ALU.mult, op1=ALU.mult), [por, gsr, onT.v()[1]], [otp_reg])

            S_ = V(SST, 128)

            def recur_prompt(R_, n, a0, hh, ocol0):
                up = Cm(3, n, 128)
                o_ = Cm(4, n, 128)
                for c in range(n // 64):
                    r0 = 64 * c
                    pw, pwc = sbank()
                    pwr = pw.v((pwc, pwc + 128))[1]
                    P.add("pe", mm(pw.h[0:n, pwc:pwc + 128], R_["wT"][0], S_[0]), [R_["wT"][1], S_[1]], [pwr])
                    dve(I("tensor_tensor", out=up[0][r0:r0 + 64, :], in0=R_["u"][0][r0:r0 + 64, :],
                          in1=pw.h[r0:r0 + 64, pwc:pwc + 128], op=ALU.subtract), [pwr, R_["u"][1]], [up[1]])
                    po, poc = sbank()
                    por = po.v((poc, poc + 128))[1]
                    P.add("pe", [mm(po.h[0:n, poc:poc + 128], R_["qdT"][0], S_[0], start=True, stop=False),
                                 mm(po.h[0:n, poc:poc + 128], R_["qkT"][0][r0:r0 + 64, :], up[0][r0:r0 + 64, :],
                                    start=False, stop=True)],
                          [R_["qdT"][1], S_[1], R_["qkT"][1], up[1]], [por])
                    dve(I("tensor_copy", out=o_[0][r0:r0 + 64, :], in_=po.h[r0:r0 + 64, poc:poc + 128]), [por], [o_[1]])
                    ps_, psc = sbank()
                    psr_ = ps_.v((psc, psc + 128))[1]
                    P.add("pe", mm(ps_.h[:, psc:psc + 128], R_["kdec"][0][r0:r0 + 64, :], up[0][r0:r0 + 64, :]),
                          [R_["kdec"][1], up[1]], [psr_])
                    dve(I("scalar_tensor_tensor", out=S_[0], in0=S_[0], scalar=R_["egl"][0][:, c:c + 1],
                          in1=ps_.h[:, psc:psc + 128], op0=ALU.mult, op1=ALU.add), [psr_, R_["egl"][1], S_[1]],
                        [S_[1]])
                o_finish(o_, n, a0, hh, ocol0)

            def recur_sample(R_, h, hh, ocol0):
                n = NS
                NBS = 2
                up = Cm(3, n, 128)
                o_ = Cm(4, n, 128)
                sbuf_ = [AR.f32(XTR + 512 + i * 256, (NBS, 128)) for i in range(2)]
                wTm = AR.f32(XTR + 0, (NBS, n))
                qdm = AR.f32(XTR + 128, (NBS, n))
                kdm = AR.f32(XTR + 256, (NBS, 128), p=(0, n))
                pw, pwc = sbank()
                pwr = pw.v((pwc, pwc + 128))[1]
                po, poc = sbank()
                por = po.v((poc, poc + 128))[1]
                nbt = NSS // NBS
                for b in range(nbt):
                    s0 = b * NBS
                    sb = sbuf_[b % 2]
                    P.dma("sp", sb[0], sd[l, s0:s0 + NBS, h].rearrange("s k v -> k s v"), ldslot(), writes=[sb[1]])
                    dve(I("tensor_tensor", out=wTm[0], in0=R_["wT"][0].unsqueeze(1).to_broadcast([128, NBS, n]),
                          in1=seqrowT.h[:, s0:s0 + NBS, :], op=ALU.mult), [R_["wT"][1]], [wTm[1]])
                    dve(I("tensor_tensor", out=qdm[0], in0=R_["qdT"][0].unsqueeze(1).to_broadcast([128, NBS, n]),
                          in1=seqrowT.h[:, s0:s0 + NBS, :], op=ALU.mult), [R_["qdT"][1]], [qdm[1]])
                    for i in range(NBS):
                        s = s0 + i
                        P.add("pe", mm(pw.h[0:n, pwc:pwc + 128], wTm[0][:, i, :], sb[0][:, i, :], start=(s == 0),
                                       stop=(s == NSS - 1)), [wTm[1], sb[1]], [pwr])
                        P.add("pe", mm(po.h[0:n, poc:poc + 128], qdm[0][:, i, :], sb[0][:, i, :], start=(s == 0),
                                       stop=False), [qdm[1], sb[1]], [por])
                dve(I("tensor_tensor", out=up[0], in0=R_["u"][0], in1=pw.h[0:n, pwc:pwc + 128], op=ALU.subtract),
                    [pwr, R_["u"][1]], [up[1]])
                P.add("pe", mm(po.h[0:n, poc:poc + 128], R_["qkT"][0], up[0], start=False, stop=True),
                      [R_["qkT"][1], up[1]], [por])
                dve(I("tensor_copy", out=o_[0], in_=po.h[0:n, poc:poc + 128]), [por], [o_[1]])
                for b in range(nbt):
                    s0 = b * NBS
                    sb = sbuf_[b % 2]
                    P.dma("sp", sb[0], sd[l, s0:s0 + NBS, h].rearrange("s k v -> k s v"), ldslot(), writes=[sb[1]])
                    dve(I("tensor_tensor", out=kdm[0], in0=R_["kdec"][0].unsqueeze(1).to_broadcast([n, NBS, 128]),
                          in1=seqcolT.h[:, s0:s0 + NBS].unsqueeze(2).to_broadcast([n, NBS, 128]), op=ALU.mult),
                        [R_["kdec"][1]], [kdm[1]])
                    for i in range(NBS):
                        s = s0 + i
                        ps_, psc = sbank()
                        psr_ = ps_.v((psc, psc + 128))[1]
                        P.add("pe", mm(ps_.h[:, psc:psc + 128], kdm[0][:, i, :], up[0]), [kdm[1], up[1]], [psr_])
                        dve(I("scalar_tensor_tensor", out=sb[0][:, i, :], in0=sb[0][:, i, :],
                              scalar=R_["egl"][0][:, s:s + 1], in1=ps_.h[:, psc:psc + 128], op0=ALU.mult,
                              op1=ALU.add), [psr_, R_["egl"][1], sb[1]], [sb[1]])
                    P.dma("sp", sd_o[l, s0:s0 + NBS, h].rearrange("s k v -> k s v"), sb[0], stslot(), reads=[sb[1]],
                          final=True)
                o_finish(o_, n, 0, hh, ocol0)

            def wout_pair(row0):
                gw, rw = wload_row(w_out[l], row0, 256)
                for dm in range(KC):
                    for (c0, c1) in dtiles_all:
                        n = c1 - c0
                        pt = dbank()
                        P.add("pe", [mm(pt.h[:, 0:n], gw(a, dm * 128, dm * 128 + 128),
                                        otp_ap[:, a * TMAX + c0:a * TMAX + c1], start=(a == 0), stop=(a == 1))
                                     for a in range(2)], [rw, otp_reg], [pt.v((0, n))[1]])
                        x_update(pt, n, dm, c0, c1, QG)

            par = [0]
            if cfg.debug >= 4:
                dve(I("memset", AR.h[:, OTP:OTP + TMAX], 0.0), [], [otp_reg])
            for hp in range(H // 2 if cfg.debug != 3 else 0):
                gq, rq = wload_col(w_in[l], OFF_Q + hp * 256, 256)
                gk, rk = wload_col(w_in[l], OFF_K + hp * 256, 256)
                gv, rv = wload_col(w_in[l], OFF_V + hp * 256, 256)
                gg, rg = wload_col(w_in[l], OFF_G + hp * 256, 256)
                conv_state_out(gq, rq, 256, pdc, sdc_o, OFF_Q + hp * 256)
                conv_state_out(gk, rk, 256, pdc, sdc_o, OFF_K + hp * 256)
                conv_state_out(gv, rv, 256, pdc, sdc_o, OFF_V + hp * 256)
                for hh in range(2):
                    h = hp * 2 + hh
                    if first:
                        dve(I("memset", S_[0], 0.0), [], [S_[1]])
                    else:
                        P.dma("sp", S_[0], sscr[l, h], ldslot(), writes=[S_[1]])
                    for (c0, c1) in dtiles_all:
                        n = c1 - c0
                        sample = c0 >= npr
                        for (get, wreg, sec, off) in ((gq, rq, 0, QS), (gk, rk, 1, KS), (gv, rv, 2, VS)):
                            pt = proj_piece(get, wreg, hh * 128, c0, c1)
                            blk = sec * 8 + h
                            outr = conv_piece(pt, n, blk, (dcwT.h[:, blk, :], dcwT.v()[1]), off, sample,
                                              st_src=sdc, st_col0=blk * 128)
                            act(AR.h[:, off:off + n], AR.h[:, off:off + n], AF.Silu, [outr], [outr])
                        pt = proj_piece(gg, rg, hh * 128, c0, c1)
                        act(AR.h[:, GS:GS + n], pt.h[:, 0:n], AF.Silu, [pt.v((0, n))[1]],
                            [("arena", GS * 4, (GS + n) * 4)])
                        l2norm(QS, n, float(DK) ** -0.5)
                        l2norm(KS, n, None)
                        dbg = cfg.debug
                        if dbg == 4:
                            continue
                        if dbg >= 50:
                            dbg = 5
                        if not sample:
                            for a0 in range(0, n, 128):
                                ti = (c0 + a0) // 128
                                R_ = prep(h, ti, 128, a0, MLE, ML, par[0] % 2)
                                par[0] += 1
                                if dbg != 5:
                                    recur_prompt(R_, 128, a0, hh, c0 + a0)
                        else:
                            R_ = prep(h, ntile_p, NS, 0, MLE4, ML4, par[0] % 2)
                            par[0] += 1
                            if dbg not in (5, 6):
                                recur_sample(R_, h, hh, c0)
                    if last:
                        P.dma("sp", pd[l, h], S_[0], stslot(), reads=[S_[1]], final=True)
                    else:
                        P.dma("sp", sscr[l, h], S_[0], stslot(), reads=[S_[1]], writes=[("sscr", (l * H + h) * 4, (l * H + h) * 4 + 4)])
                wout_pair(hp * 256)

            def lru_wload(bp):
                def pa(s):
                    return (ring.h[:, s, 0:256].rearrange("p (n j) -> p n j", j=128),
                            lru_wa[l, 2 * bp:2 * bp + 2].rearrange("n i j -> i n j"))

                def pb(s):
                    return (ring.h[:, s, 256:512].rearrange("p (n j) -> p n j", j=128),
                            lru_wx[l, 2 * bp:2 * bp + 2].rearrange("n i j -> i n j"))
                s = wload([pa, pb])
                return s, ring.v(s)[1]
            XC, RR, II, HH_, XB = QS, KS, VS, GS, SQ
            for bp in range(NB // 2 if cfg.debug in (0, 1, 3) else 0):
                gx, rx = wload_col(w_in[l], OFF_X + bp * 256, 256)
                gy, ry = wload_col(w_in[l], OFF_Y + bp * 256, 256)
                sl_, rlw = lru_wload(bp)
                conv_state_out(gx, rx, 256, plc, slc_o, bp * 256)
                for bb in range(2):
                    nb = bp * 2 + bb
                    blk = 24 + nb
                    for (c0, c1) in dtiles_all:
                        n = c1 - c0
                        sample = c0 >= npr
                        pt = proj_piece(gx, rx, bb * 128, c0, c1)
                        xcr = conv_piece(pt, n, blk, (lcwT.h[:, nb, :], lcwT.v()[1]), XC, sample,
                                         bias=(lprm.h[:, 0, nb:nb + 1], lprm.v()[1]), st_src=slc, st_col0=nb * 128)
                        xc = AR.h[:, XC:XC + n]
                        xb, xbr = AR.bf16(RAW, (n,))
                        P.add("act", I("copy", out=xb, in_=xc), [xcr], [xbr])
                        pr = dbank()
                        P.add("pe", mm(pr.h[:, 0:n], ring.h[:, sl_, bb * 128:(bb + 1) * 128], xb), [rlw, xbr],
                              [pr.v((0, n))[1]])
                        pi_ = dbank()
                        P.add("pe", mm(pi_.h[:, 0:n], ring.h[:, sl_, (2 + bb) * 128:(2 + bb + 1) * 128], xb),
                              [rlw, xbr], [pi_.v((0, n))[1]])
                        rr, rrr = V(RR, n)
                        ii, iir = V(II, n)
                        hhv, hhr = V(HH_, n)
                        sq, sqr = V(XB, n)
                        act(rr, pr.h[:, 0:n], AF.Sigmoid, [pr.v((0, n))[1], lprm.v()[1]], [rrr],
                            bias=lprm.h[:, 1, nb:nb + 1])
                        act(rr, rr, AF.Exp, [rrr, lprm.v()[1]], [rrr], scale=lprm.h[:, 3, nb:nb + 1])
                        act(ii, pi_.h[:, 0:n], AF.Sigmoid, [pi_.v((0, n))[1], lprm.v()[1]], [iir],
                            bias=lprm.h[:, 2, nb:nb + 1])
                        dve(I("tensor_tensor", out=ii, in0=ii, in1=xc, op=ALU.mult), [iir, xcr], [iir])
                        dve(I("tensor_tensor", out=sq, in0=rr, in1=rr, op=ALU.mult), [rrr], [sqr])
                        act(sq, sq, AF.Sqrt, [sqr], [sqr], bias=epsc.h[:, 1:2], scale=-1.0)
                        dve(I("tensor_tensor", out=ii, in0=ii, in1=sq, op=ALU.mult), [iir, sqr], [iir])
                        if not sample:
                            dve(I("tensor_tensor_scan", out=hhv, data0=rr, data1=ii,
                                  initial=carry_h.h[:, l, nb:nb + 1], op0=ALU.mult, op1=ALU.add),
                                [rrr, iir, carry_h.v()[1]], [hhr])
                            dve(I("tensor_copy", out=carry_h.h[:, l, nb:nb + 1], in_=AR.h[:, HH_ + n - 1:HH_ + n]),
                                [hhr], [carry_h.v()[1]])
                            if last and c1 == npr:
                                P.dma("sp", pl[l, nb * 128:(nb + 1) * 128].rearrange("(p o) -> p o", o=1),
                                      AR.h[:, HH_ + n - 1:HH_ + n], stslot(), reads=[hhr], final=True)
                        else:
                            stg, stgr = AR.f32(XTR, (128,), p=(0, NSS))
                            P.dma("sp", stg, sl[l][:, nb * 128:(nb + 1) * 128], ldslot(), writes=[stgr])
                            p2, pc2 = sbank()
                            p2r = p2.v((pc2, pc2 + 128))[1]
                            P.add("pe", tr(p2.h[:, pc2:pc2 + NSS], stg, CST.h[0:NSS, 0, 0:NSS]), [stgr], [p2r])
                            h3 = hhv.rearrange("p (s t) -> p s t", t=LS)
                            a3 = rr.rearrange("p (s t) -> p s t", t=LS)
                            b3 = ii.rearrange("p (s t) -> p s t", t=LS)
                            for t in range(LS):
                                prev = p2.h[:, pc2:pc2 + NSS] if t == 0 else h3[:, :, t - 1]
                                rd = [p2r] if t == 0 else []
                                dve(I("tensor_tensor", out=h3[:, :, t], in0=a3[:, :, t], in1=prev, op=ALU.mult),
                                    [rrr, hhr] + rd, [hhr])
                                dve(I("tensor_tensor", out=h3[:, :, t], in0=h3[:, :, t], in1=b3[:, :, t],
                                      op=ALU.add), [iir, hhr], [hhr])
                            p3, pc3 = sbank()
                            p3r = p3.v((pc3, pc3 + 128))[1]
                            hl, hlr = AR.f32(XTR + 128, (NSS,))
                            dve(I("tensor_copy", out=hl, in_=h3[:, :, LS - 1]), [hhr], [hlr])
                            P.add("pe", tr(p3.h[0:NSS, pc3:pc3 + 128], hl, IDENT[0]), [hlr], [p3r])
                            so, sor = AR.f32(XTR + 256, (128,), p=(0, NSS))
                            dve(I("tensor_copy", out=so, in_=p3.h[0:NSS, pc3:pc3 + 128]), [p3r], [sor])
                            P.dma("sp", sl_o[l][:, nb * 128:(nb + 1) * 128], so, stslot(), reads=[sor], final=True)
                        py = proj_piece(gy, ry, bb * 128, c0, c1)
                        act(sq, py.h[:, 0:n], AF.Gelu_apprx_tanh, [py.v((0, n))[1]], [sqr])
                        dve(I("tensor_tensor", out=otp_ap[:, bb * TMAX + c0:bb * TMAX + c1], in0=hhv, in1=sq,
                              op=ALU.mult), [hhr, sqr], [otp_reg])
                wout_pair((8 + bp * 2) * 128)

        setup()
        for pi in range(len(cfg.passes)):
            run_pass(pi)

        block = es.enter_context(nc.Block())
        P.emit(nc, block, esem)
    return nc


def make_consts():
    c = np.zeros((128, 8, 128), np.float32)
    i = np.arange(128)
    c[:, 0, :] = np.eye(128)
    c[:, 1, :] = 1.0
    same64 = (i[:, None] // 64) == (i[None, :] // 64)
    c[:, 2, :] = same64 & (i[:, None] <= i[None, :])
    c[:, 3, :] = same64 & (i[:, None] > i[None, :])
    c[:, 4, :] = same64
    same4 = ((i[:, None] // 4) == (i[None, :] // 4)) & (i[:, None] < 64) & (i[None, :] < 64)
    c[:, 5, :] = same4 & (i[:, None] <= i[None, :])
    c[:, 6, :] = same4 & (i[:, None] > i[None, :])
    c[:, 7, :] = same4
    t = np.arange(NS)
    seqrow = (t[None, :] // LS == np.arange(NSS)[:, None]).astype(np.float32).reshape(-1)
    seqcol = (t[:, None] // LS == np.arange(NSS)[None, :]).astype(np.float32)
    return c, seqrow, seqcol


_PROG_CACHE = {}


def run_cfg(cfg, inp, n_cores=8, n_prompt=4):
    key = (cfg.depth, cfg.passes, cfg.debug)
    if key not in _PROG_CACHE:
        _PROG_CACHE[key] = build_program(cfg)
    nc = _PROG_CACHE[key]
    cst, seqrow, seqcol = make_consts()
    f = lambda a: np.ascontiguousarray(np.asarray(a, dtype=np.float32))
    shared = {k: f(inp[k]) for k in ("w_in", "w_out", "norm_g", "w_ada", "b_ada", "ffn_w1", "ffn_w3", "ffn_w2",
                                     "dconv_w", "d_alog", "d_dtbias", "d_onorm", "lconv_w", "lconv_b", "lru_wa",
                                     "lru_ba", "lru_wx", "lru_bx", "lru_lam", "final_g")}
    shared.update(cst=cst, seqrow=seqrow, seqcol=seqcol)
    if cfg.debug:
        for k_ in ("w_ada", "ffn_w1", "ffn_w3", "ffn_w2"):
            shared[k_] = np.zeros((1, 128, 128) if k_ == "w_ada" else (1, 1, 128, 128), np.float32)
    Ld = cfg.depth
    in_maps = []
    for c in range(n_cores):
        p = c % n_prompt
        s0, s1 = c * NSS, (c + 1) * NSS
        m = dict(shared)
        m["xp"] = f(inp["x_prompt"][p])
        m["xs"] = f(inp["x_sample"][s0:s1]).reshape(NS, D)
        m["c"] = f(np.concatenate([inp["c_prompt"][p:p + 1], inp["c_sample"][s0:s1]], axis=0))
        m["sd"] = f(inp["state_delta"][:, s0:s1])
        m["sdc"] = f(inp["state_delta_conv"][:, s0:s1]).reshape(Ld, NSS * 3, NDC)
        m["sl"] = f(inp["state_lru"][:, s0:s1])
        m["slc"] = f(inp["state_lru_conv"][:, s0:s1]).reshape(Ld, NSS * 3, WB)
        in_maps.append(m)
    res = run_bass_kernel_spmd(nc, in_maps, core_ids=list(range(n_cores)))
    return res.results


def assemble(results, n_prompt=4):
    cat = lambda k, ax: np.concatenate([r[k] for r in results], axis=ax)
    stack_p = lambda k: np.stack([results[p][k] for p in range(n_prompt)], axis=1)
    y_prompt = np.stack([results[p]["yp"] for p in range(n_prompt)], axis=0)
    y_sample = cat("ys", 0).reshape(len(results) * NSS, LS, D)
    return (y_prompt, y_sample, stack_p("pd"), stack_p("pdc"), stack_p("pl"), stack_p("plc"),
            cat("sd_o", 1), cat("sdc_o", 1), cat("sl_o", 1), cat("slc_o", 1))


def kernel(**inputs):
    cfg = Cfg()
    results = run_cfg(cfg, inputs)
    return tuple(np.ascontiguousarray(a, dtype=np.float32) for a in assemble(results))
```

```python
from contextlib import ExitStack

import numpy as np

import concourse.bass as bass
import concourse.mybir as mybir
from concourse.bass_utils import run_bass_kernel_spmd

F32 = mybir.dt.float32
BF16 = mybir.dt.bfloat16
AF = mybir.ActivationFunctionType
ALU = mybir.AluOpType
AX = mybir.AxisListType

D = 2048
KC = 16
H = 8
DK = 128
DV = 128
WB = 1024
NB = 8
DFF = 5504
FCN = 43
OFF_Q, OFF_K, OFF_V, OFF_G, OFF_A, OFF_B, OFF_X, OFF_Y, N_IN = 0, 1024, 2048, 3072, 4096, 4104, 4112, 5136, 6160
NDC = 3072
EPS = 1e-6
NSS = 16
LS = 4
NS = NSS * LS
NSLOT = 6
SLOT = 4096

ENGS = ("pe", "act", "dve", "pool", "sp")


def I(name, *args, **kw):
    return lambda e: getattr(e, name)(*args, **kw)


class Slot:
    def __init__(self, sem):
        self.sem = sem
        self.count = 0


class Op:
    __slots__ = ("eng", "fns", "deps", "marked", "rank", "dma")

    def __init__(self, eng, fns, deps, dma=None):
        self.eng, self.fns, self.deps, self.marked, self.rank, self.dma = eng, fns, deps, False, 0, dma


class Prog:
    def __init__(self):
        self.ops = {e: [] for e in ENGS}
        self.track = {}
        self.final = []

    def _touch(self, reg, is_write, ref, eng, deps):
        tid, lo, hi = reg
        if tid[0] == "p" and tid[1] == "s" and len(tid) == 3:
            lo, hi = 0, 2048
        lst = self.track.get(tid)
        if lst is None:
            lst = self.track[tid] = []
        new = []
        for rec in lst:
            rlo, rhi, rw, rref, reng = rec
            if rlo < hi and lo < rhi:
                if (is_write or rw) and rref is not ref:
                    deps.add(rref)
                if lo <= rlo and rhi <= hi and (is_write or ((not rw) and reng == eng)):
                    continue
            new.append(rec)
        new.append((lo, hi, is_write, ref, eng))
        self.track[tid] = new

    def add(self, eng, fns, reads=(), writes=()):
        if not isinstance(fns, (list, tuple)):
            fns = [fns]
        idx = len(self.ops[eng])
        ref = ("e", eng, idx)
        deps = set()
        for r in reads:
            self._touch(r, False, ref, eng, deps)
        for w in writes:
            self._touch(w, True, ref, eng, deps)
        op = Op(eng, list(fns), self._filter(eng, deps))
        self.ops[eng].append(op)
        return ref

    def _filter(self, eng, deps):
        out = []
        for d in deps:
            if d[0] == "e":
                if d[1] == "pe" and eng == "pe":
                    continue
                self.ops[d[1]][d[2]].marked = True
            out.append(d)
        return out

    def dma(self, eng, out_ap, in_ap, slot, reads=(), writes=(), final=False, slow=False):
        idx = len(self.ops[eng])
        prev = ("d", slot, slot.count) if slot.count else None
        slot.count += 16
        ref = ("d", slot, slot.count)
        deps = set()
        if prev is not None:
            deps.add(prev)
        for r in reads:
            self._touch(r, False, ref, ref, deps)
        for w in writes:
            self._touch(w, True, ref, ref, deps)
        kw = {"allow_slow_non_contiguous": True} if slow else {}
        op = Op(eng, [I("dma_start", out=out_ap, in_=in_ap, **kw)], self._filter("dma", deps), dma=slot)
        self.ops[eng].append(op)
        if final:
            self.final.append(ref)
        return ref

    def emit(self, nc, block, esem):
        for e in ENGS:
            r = 0
            for op in self.ops[e]:
                if op.marked:
                    r += 1
                    op.rank = r
        prog = self

        def resolve(d):
            if d[0] == "e":
                return esem[d[1]], prog.ops[d[1]][d[2]].rank
            return d[1].sem, d[2]

        def run(engname, engobj_name):
            def body(eng):
                waited = {}
                ops = prog.ops[engname]
                for op in ops:
                    need = {}
                    for d in op.deps:
                        sem, val = resolve(d)
                        k = id(sem)
                        if waited.get(k, 0) < val and need.get(k, (None, 0))[1] < val:
                            need[k] = (sem, val)
                    items = list(need.values())
                    for k, (sem, val) in need.items():
                        waited[k] = val
                    if op.dma is not None:
                        for sem, val in items:
                            eng.wait_ge(sem, val)
                        op.fns[0](eng).then_inc(op.dma.sem, 16)
                        continue
                    for sem, val in items[:-1]:
                        eng.wait_ge(sem, val)
                    last = None
                    for i, fn in enumerate(op.fns):
                        ins = fn(eng)
                        if i == 0 and items:
                            ins._wait_ge(items[-1][0], items[-1][1])
                        last = ins
                    if op.marked:
                        last.then_inc(esem[engname], 1)
                if engname == "sp":
                    for d in prog.final:
                        sem, val = resolve(d)
                        eng.wait_ge(sem, val)
            return body

        block.tensor(run("pe", None))
        block.scalar(run("act", None))
        block.vector(run("dve", None))
        block.gpsimd(run("pool", None))
        block.sync(run("sp", None))


class TT:
    def __init__(self, h, name, shape, esz):
        self.h, self.name, self.shape, self.esz = h, name, tuple(shape), esz
        st = []
        s = 1
        for n in reversed(self.shape[1:]):
            st.append(s)
            s *= n
        self.strides = tuple(reversed(st))

    def v(self, *idx, p=None):
        key = [slice(None) if p is None else slice(p[0], p[1])]
        lo = hi = 0
        for i, n in enumerate(self.shape[1:]):
            ix = idx[i] if i < len(idx) else None
            if ix is None:
                a, b = 0, n
                key.append(slice(None))
            elif isinstance(ix, tuple):
                a, b = ix
                key.append(slice(a, b))
            else:
                a, b = ix, ix + 1
                key.append(ix)
            lo += a * self.strides[i]
            hi += (b - 1) * self.strides[i]
        return self.h[tuple(key)], (self.name, lo * self.esz, (hi + 1) * self.esz)


class Arena:
    def __init__(self, h, words):
        self.h, self.words = h, words

    def f32(self, off, shape, p=None):
        n = int(np.prod(shape))
        assert off + n <= self.words, (off, n, self.words)
        ps = slice(None) if p is None else slice(p[0], p[1])
        ap = self.h[ps, off:off + n]
        if len(shape) == 2:
            ap = ap.rearrange("p (a b) -> p a b", b=shape[1])
        elif len(shape) == 3:
            ap = ap.rearrange("p (a b c) -> p a b c", b=shape[1], c=shape[2])
        return ap, ("arena", off * 4, (off + n) * 4)

    def bf16(self, off, shape, p=None):
        n = int(np.prod(shape))
        w = (n + 1) // 2
        assert off + w <= self.words, (off, w, self.words)
        ps = slice(None) if p is None else slice(p[0], p[1])
        ap = self.h[ps, off:off + w].bitcast(BF16)[:, 0:n]
        if len(shape) == 2:
            ap = ap.rearrange("p (a b) -> p a b", b=shape[1])
        return ap, ("arena", off * 4, (off + w) * 4)


class Cfg:
    def __init__(self, depth=4, passes=((1024, True), (1024, False)), out_prompt=True, debug=0):
        self.debug = debug
        self.depth = depth
        self.passes = passes
        self.seq = sum(p[0] for p in passes)
        self.tmax = max(p[0] + (NS if p[1] else 0) for p in passes)


def build_program(cfg):
    nc = bass.Bass("TRN2", target_bir_lowering=False)
    L = cfg.depth
    SEQ = cfg.seq
    TMAX = cfg.tmax

    def din(name, shape):
        return nc.dram_tensor(name, list(shape), F32, kind="ExternalInput").ap()

    def dout(name, shape):
        return nc.dram_tensor(name, list(shape), F32, kind="ExternalOutput").ap()

    xp = din("xp", (SEQ, D))
    xs = din("xs", (NS, D))
    cin = din("c", (1 + NSS, D))
    sd = din("sd", (L, NSS, H, DK, DV))
    sdc = din("sdc", (L, NSS * 3, NDC))
    sl = din("sl", (L, NSS, WB))
    slc = din("slc", (L, NSS * 3, WB))
    w_in = din("w_in", (L, D, N_IN))
    w_out = din("w_out", (L, D, D))
    norm_g = din("norm_g", (L, 3, D))
    w_ada = din("w_ada", (L, D, 9 * D) if not cfg.debug else (1, 128, 128))
    b_ada = din("b_ada", (L, 9 * D))
    fshape = (L, 2, D, DFF) if not cfg.debug else (1, 1, 128, 128)
    ffn_w1 = din("ffn_w1", fshape)
    ffn_w3 = din("ffn_w3", fshape)
    ffn_w2 = din("ffn_w2", (L, 2, DFF, D) if not cfg.debug else (1, 1, 128, 128))
    dconv_w = din("dconv_w", (L, 4, NDC))
    d_alog = din("d_alog", (L, H))
    d_dtbias = din("d_dtbias", (L, H))
    d_onorm = din("d_onorm", (L, DV))
    lconv_w = din("lconv_w", (L, 4, WB))
    lconv_b = din("lconv_b", (L, WB))
    lru_wa = din("lru_wa", (L, NB, 128, 128))
    lru_ba = din("lru_ba", (L, WB))
    lru_wx = din("lru_wx", (L, NB, 128, 128))
    lru_bx = din("lru_bx", (L, WB))
    lru_lam = din("lru_lam", (L, WB))
    final_g = din("final_g", (D,))
    cst = din("cst", (128, 8, 128))
    seqrow = din("seqrow", (NSS * NS,))
    seqcol = din("seqcol", (NS, NSS))

    yp = dout("yp", (SEQ, D))
    ys = dout("ys", (NS, D))
    pd = dout("pd", (L, H, DK, DV))
    pdc = dout("pdc", (L, 3, NDC))
    pl = dout("pl", (L, WB))
    plc = dout("plc", (L, 3, WB))
    sd_o = dout("sd_o", (L, NSS, H, DK, DV))
    sdc_o = dout("sdc_o", (L, NSS, 3, NDC))
    sl_o = dout("sl_o", (L, NSS, WB))
    slc_o = dout("slc_o", (L, NSS, 3, WB))
    sscr = nc.dram_tensor("sscr", [L, H, DK, DV], F32, kind="Internal").ap()

    P = Prog()
    es = ExitStack()
    with es:
        def sbuf(name, shape, dt=F32):
            h = es.enter_context(nc.sbuf_tensor(name, list(shape), dt))
            return TT(h, name, shape, 4 if dt == F32 else 2)

        xT = sbuf("xT", (128, KC, TMAX))
        hT = sbuf("hT", (128, KC, TMAX), BF16)
        ring = sbuf("ring", (128, NSLOT, SLOT), BF16)
        CST = sbuf("CST", (128, 8, 128))
        onesb = sbuf("onesb", (128, 128), BF16)
        seqrowT = sbuf("seqrowT", (128, NSS, NS))
        seqcolT = sbuf("seqcolT", (NS, NSS))
        modT = sbuf("modT", (128, 144, 1 + NSS))
        modsave = sbuf("modsave", (128, L, 144))
        csT = sbuf("csT", (128, KC, 1 + NSS), BF16)
        ngT = sbuf("ngT", (128, 3, KC))
        badaT = sbuf("badaT", (128, 144))
        fgT = sbuf("fgT", (128, KC))
        dcwT = sbuf("dcwT", (128, 24, 4))
        lcwT = sbuf("lcwT", (128, 8, 4))
        lprm = sbuf("lprm", (128, 5, 8))
        onT = sbuf("onT", (128, 1))
        rowp = sbuf("rowp", (128, 2, 8))
        epsc = sbuf("epsc", (128, 2))
        carry_c = sbuf("carry_c", (128, L, 32, 3))
        carry_h = sbuf("carry_h", (128, L, 8))
        AW = 8448
        arena_t = sbuf("arena", (128, AW))
        AR = Arena(arena_t.h, AW)
        PS = []
        for i in range(8):
            h = es.enter_context(nc.psum_tensor(f"ps{i}", [128, 512], F32))
            PS.append(TT(h, f"ps{i}", (128, 512), 4))
        esem = {e: es.enter_context(nc.semaphore("sem_" + e)) for e in ENGS}

        def newslot(name):
            return Slot(es.enter_context(nc.semaphore(name)))

        ring_slots = [newslot(f"ring{i}") for i in range(NSLOT)]
        ld_slots = [newslot(f"ld{i}") for i in range(8)]
        st_slots = [newslot(f"st{i}") for i in range(8)]
        st_i = [0]
        ld_i = [0]

        def ldslot():
            ld_i[0] += 1
            return ld_slots[ld_i[0] % len(ld_slots)]

        def stslot():
            st_i[0] += 1
            return st_slots[st_i[0] % len(st_slots)]

        IDENT = CST.v(0)
        ONESF = CST.v(1)
        MLE = CST.v(2)
        ML = CST.v(3)
        BONES = CST.v(4)
        MLE4 = CST.v(5)
        ML4 = CST.v(6)
        BONES4 = CST.v(7)
        NOREG = ()

        def act(out, in_, func, reads, writes, bias=None, scale=None):
            kw = {}
            if bias is not None:
                kw["bias"] = bias
            if scale is not None:
                kw["scale"] = scale
            P.add("act", I("activation", out=out, in_=in_, func=func, **kw), reads, writes)

        def dve(fn, reads, writes):
            P.add("dve", fn, reads, writes)

        ring_n = [0]

        def wload(parts):
            s = ring_n[0] % NSLOT
            ring_n[0] += 1
            for (dst_ap, src_ap) in [pp(s) for pp in parts]:
                P.dma("pool", dst_ap, src_ap, ring_slots[s], writes=[ring.v(s)[1]])
            return s

        def wload_col(wmat, c0, ncol):
            src = wmat.rearrange("(k p) n -> p k n", p=128)[:, :, c0:c0 + ncol]

            def part(s):
                dst = ring.h[:, s, 0:KC * ncol].rearrange("p (k n) -> p k n", n=ncol)
                return dst, src
            s = wload([part])
            reg = ring.v(s)[1]

            my = ring_n[0]

            def get(kc, j0, j1):
                assert ring_n[0] - my < NSLOT, "ring slot reused while still referenced"
                return ring.h[:, s, kc * ncol + j0:kc * ncol + j1]
            return get, reg

        def wload_row(wmat, r0, nrow):
            na = nrow // 128
            src = wmat[r0:r0 + nrow, :].rearrange("(a p) n -> p a n", p=128)

            def part(s):
                dst = ring.h[:, s, 0:na * D].rearrange("p (a n) -> p a n", n=D)
                return dst, src
            s = wload([part])
            reg = ring.v(s)[1]

            my = ring_n[0]

            def get(a, j0, j1):
                assert ring_n[0] - my < NSLOT, "ring slot reused while still referenced"
                return ring.h[:, s, a * D + j0:a * D + j1]
            return get, reg

        ps_i = [0]

        def dbank():
            ps_i[0] += 1
            return PS[ps_i[0] % 4]

        sps_i = [0]

        def sbank():
            sps_i[0] += 1
            k = sps_i[0] % 4
            return PS[4 + k], 0

        def mm(out, lhsT, rhs, start=True, stop=True):
            return I("matmul", out, lhsT=lhsT, rhs=rhs, start=start, stop=stop)

        def tr(out, in_, ident):
            return I("transpose", out, in_, ident)

        def setup():
            sl0 = ldslot()
            P.dma("sp", CST.h[:], cst, sl0, writes=[CST.v()[1]])
            P.dma("sp", seqrowT.h[:].rearrange("p a b -> p (a b)"), seqrow.partition_broadcast(128), sl0,
                  writes=[seqrowT.v()[1]])
            P.dma("sp", seqcolT.h[:], seqcol, sl0, writes=[seqcolT.v()[1]])
            P.dma("sp", fgT.h[:], final_g.rearrange("(k p) -> p k", p=128), sl0, writes=[fgT.v()[1]], slow=True)
            dve(I("memset", onesb.h[:], 1.0), [], [onesb.v()[1]])
            dve(I("memset", epsc.h[:, 0:1], EPS), [], [epsc.v()[1]])
            dve(I("memset", epsc.h[:, 1:2], 1.0), [], [epsc.v()[1]])
            dve(I("memset", carry_c.h[:], 0.0), [], [carry_c.v()[1]])
            dve(I("memset", carry_h.h[:], 0.0), [], [carry_h.v()[1]])
            ctok, ctr = AR.f32(0, (D,), p=(0, 1 + NSS))
            P.dma("sp", ctok, cin, sl0, writes=[ctr])
            act(ctok, ctok, AF.Silu, [ctr], [ctr])
            for g4 in range(4):
                pt = dbank()
                fns = []
                for j in range(4):
                    kc = g4 * 4 + j
                    fns.append(tr(pt.h[:, j * 32:j * 32 + 1 + NSS], ctok[:, kc * 128:(kc + 1) * 128],
                                  CST.h[0:1 + NSS, 0, 0:1 + NSS]))
                P.add("pe", fns, [ctr, IDENT[1]], [pt.v()[1]])
                dve(I("tensor_copy",
                    out=csT.h[:, g4 * 4:g4 * 4 + 4, :],
                    in_=pt.h[:, 0:128].rearrange("p (a b) -> p a b", b=32)[:, :, 0:1 + NSS]),
                    [pt.v()[1]], [csT.v()[1]])

        def layer_params(l):
            s0 = ldslot()
            with nc.allow_non_contiguous_dma(reason="small transposed parameter loads"):
                for s_ in range(3):
                    P.dma("sp", ngT.h[:, s_, :], norm_g[l, s_].rearrange("(k p) -> p k", p=128), s0, writes=[ngT.v()[1]], slow=True)
                P.dma("sp", badaT.h[:], b_ada[l].rearrange("(q p) -> p q", p=128), s0, writes=[badaT.v()[1]], slow=True)
                for j_ in range(4):
                    P.dma("sp", dcwT.h[:, :, j_], dconv_w[l, j_].rearrange("(b p) -> p b", p=128), s0, writes=[dcwT.v()[1]], slow=True)
                    P.dma("sp", lcwT.h[:, :, j_], lconv_w[l, j_].rearrange("(b p) -> p b", p=128), s0, writes=[lcwT.v()[1]], slow=True)
                P.dma("sp", lprm.h[:, 0, :], lconv_b[l].rearrange("(b p) -> p b", p=128), s0, writes=[lprm.v()[1]], slow=True)
                P.dma("sp", lprm.h[:, 1, :], lru_ba[l].rearrange("(b p) -> p b", p=128), s0, writes=[lprm.v()[1]], slow=True)
                P.dma("sp", lprm.h[:, 2, :], lru_bx[l].rearrange("(b p) -> p b", p=128), s0, writes=[lprm.v()[1]], slow=True)
                P.dma("sp", lprm.h[:, 3, :], lru_lam[l].rearrange("(b p) -> p b", p=128), s0, writes=[lprm.v()[1]], slow=True)
                P.dma("sp", onT.h[:], d_onorm[l].rearrange("(p o) -> p o", o=1), s0, writes=[onT.v()[1]])
                P.dma("sp", rowp.h[:, 0, :], d_dtbias[l].partition_broadcast(128), s0, writes=[rowp.v()[1]])
                P.dma("sp", rowp.h[:, 1, :], d_alog[l].partition_broadcast(128), s0, writes=[rowp.v()[1]])
            act(rowp.h[:, 1, :], rowp.h[:, 1, :], AF.Exp, [rowp.v()[1]], [rowp.v()[1]])
            dve(I("tensor_scalar", out=rowp.h[:, 1, :], in0=rowp.h[:, 1, :], scalar1=-1.0, scalar2=None,
                                          op0=ALU.mult), [rowp.v()[1]], [rowp.v()[1]])
            act(lprm.h[:, 3, :], lprm.h[:, 3, :], AF.Exp, [lprm.v()[1]], [lprm.v()[1]], scale=-1.0)
            act(lprm.h[:, 3, :], lprm.h[:, 3, :], AF.Ln, [lprm.v()[1]], [lprm.v()[1]], bias=epsc.h[:, 1:2])
            dve(I("tensor_scalar", out=lprm.h[:, 3, :], in0=lprm.h[:, 3, :], scalar1=-8.0, scalar2=None,
                                          op0=ALU.mult), [lprm.v()[1]], [lprm.v()[1]])

        def ada(l):
            QB = 24
            pt = None
            for ct in range(72):
                get, wreg = wload_col(w_ada[l], ct * 256, 256)
                for j in range(2):
                    q = ct * 2 + j
                    if q % QB == 0:
                        pt = dbank()
                    o = (q % QB) * (1 + NSS)
                    fns = [mm(pt.h[:, o:o + 1 + NSS], get(kc, j * 128, (j + 1) * 128), csT.h[:, kc, :],
                              start=(kc == 0), stop=(kc == KC - 1)) for kc in range(KC)]
                    P.add("pe", fns, [wreg, csT.v()[1]], [pt.v()[1]])
                    if q % QB == QB - 1:
                        q0 = q - QB + 1
                        dve(I("tensor_tensor",
                            out=modT.h[:, q0:q0 + QB, :],
                            in0=pt.h[:, 0:QB * (1 + NSS)].rearrange("p (a b) -> p a b", b=1 + NSS),
                            in1=badaT.h[:, q0:q0 + QB].unsqueeze(2).to_broadcast([128, QB, 1 + NSS]), op=ALU.add),
                            [pt.v()[1], badaT.v()[1]], [modT.v()[1]])
            mreg = modT.v()[1]
            for sub in range(3):
                qs = sub * 48 + 16
                dve(I("tensor_scalar", out=modT.h[:, qs:qs + 16, :], in0=modT.h[:, qs:qs + 16, :],
                                                     scalar1=1.0, scalar2=None, op0=ALU.add), [mreg], [mreg])
                dve(I("tensor_tensor",
                    out=modT.h[:, qs:qs + 16, :], in0=modT.h[:, qs:qs + 16, :],
                    in1=ngT.h[:, sub, :].unsqueeze(2).to_broadcast([128, 16, 1 + NSS]), op=ALU.mult),
                    [mreg, ngT.v()[1]], [mreg])
                if sub != 1:
                    qg = sub * 48 + 32
                    dve(I("tensor_scalar", out=modT.h[:, qg:qg + 16, :], in0=modT.h[:, qg:qg + 16, :],
                                                         scalar1=0.5, scalar2=None, op0=ALU.mult), [mreg], [mreg])
            dve(I("tensor_copy", out=modsave.h[:, l, :], in_=modT.h[:, :, 0]), [mreg], [modsave.v()[1]])

        def ada_restore(l):
            dve(I("tensor_copy", out=modT.h[:, :, 0], in_=modsave.h[:, l, :]), [modsave.v()[1]],
                [modT.v()[1]])

        class PassCtx:
            pass

        def run_pass(pi):
            npr, has_s = cfg.passes[pi]
            t0 = sum(p[0] for p in cfg.passes[:pi])
            last = pi == len(cfg.passes) - 1
            first = pi == 0
            T = npr + (NS if has_s else 0)
            dtiles = [(c, min(c + 512, npr)) for c in range(0, npr, 512)]
            if has_s:
                dtiles_all = dtiles + [(npr, T)]
            else:
                dtiles_all = list(dtiles)
            mtiles = [(c, c + 128) for c in range(0, npr, 128)]
            assert npr % 128 == 0
            SC0 = npr

            xreg = lambda kc, c0, c1: xT.v(kc, (c0, c1))[1]
            hreg = lambda kc, c0, c1: hT.v(kc, (c0, c1))[1]

            def load_x():
                tiles = [(xp, t0 + c0, c0, 128) for (c0, c1) in mtiles]
                if has_s:
                    tiles.append((xs, 0, SC0, NS))
                for i, (src, r0, c0, n) in enumerate(tiles):
                    stg, sreg = AR.f32((i % 2) * D, (D,), p=(0, n))
                    P.dma("sp", stg, src[r0:r0 + n, :], ldslot(), writes=[sreg])
                    for g4 in range(4):
                        pt = dbank()
                        fns = [tr(pt.h[:, j * 128:j * 128 + n], stg[:, (g4 * 4 + j) * 128:(g4 * 4 + j + 1) * 128],
                                  CST.h[0:n, 0, 0:n]) for j in range(4)]
                        P.add("pe", fns, [sreg], [pt.v()[1]])
                        o_ap = xT.h[:, g4 * 4:g4 * 4 + 4, c0:c0 + n]
                        i_ap = pt.h[:, :].rearrange("p (a b) -> p a b", b=128)[:, :, 0:n]
                        wr = [xreg(g4 * 4 + j, c0, c0 + n) for j in range(4)]
                        if g4 % 2 == 0:
                            dve(I("tensor_copy", out=o_ap, in_=i_ap), [pt.v()[1]], wr)
                        else:
                            P.add("act", I("copy", out=o_ap, in_=i_ap), [pt.v()[1]], wr)

            NRM0 = AW - 2560

            def rstd_tile(c0, c1, k):
                n = c1 - c0
                pt = dbank()
                for kc in range(KC):
                    sq, sqr = AR.bf16(NRM0 + (kc % 2) * 256, (n,))
                    act(sq, xT.h[:, kc, c0:c1], AF.Square, [xreg(kc, c0, c1)], [sqr])
                    P.add("pe", mm(pt.h[:, 0:n], onesb.h[:], sq, start=(kc == 0), stop=(kc == KC - 1)),
                          [sqr], [pt.v((0, n))[1]])
                rs, rsr = AR.f32(NRM0 + 512 + (k % 2) * 512, (n,))
                act(rs, pt.h[:, 0:n], AF.Sqrt, [pt.v((0, n))[1]], [rsr], bias=epsc.h[:, 0:1], scale=1.0 / D)
                dve(I("reciprocal", out=rs, in_=rs), [rsr], [rsr])
                return rs, rsr

            def norm_mod(sub):
                qsh, qsc = sub * 48, sub * 48 + 16
                for k, (c0, c1) in enumerate(dtiles_all):
                    n = c1 - c0
                    rs, rsr = rstd_tile(c0, c1, k)
                    for kc in range(KC):
                        tmp, tr_ = AR.f32(NRM0 + 1536 + (kc % 2) * 512, (n,))
                        dve(I("tensor_tensor", out=tmp, in0=xT.h[:, kc, c0:c1], in1=rs,
                                                                      op=ALU.mult),
                            [xreg(kc, c0, c1), rsr], [tr_])
                        if c0 < npr:
                            act(hT.h[:, kc, c0:c1], tmp, AF.Identity, [tr_, modT.v()[1]], [hreg(kc, c0, c1)],
                                bias=modT.h[:, qsh + kc, 0:1], scale=modT.h[:, qsc + kc, 0:1])
                        else:
                            t3 = tmp.rearrange("p (s t) -> p s t", t=LS)
                            dve(I("tensor_tensor",
                                out=t3, in0=t3, in1=modT.h[:, qsc + kc, 1:1 + NSS].unsqueeze(2).to_broadcast(
                                    [128, NSS, LS]), op=ALU.mult), [tr_, modT.v()[1]], [tr_])
                            dve(I("tensor_tensor",
                                out=hT.h[:, kc, c0:c1].rearrange("p (s t) -> p s t", t=LS), in0=t3,
                                in1=modT.h[:, qsh + kc, 1:1 + NSS].unsqueeze(2).to_broadcast([128, NSS, LS]),
                                op=ALU.add), [tr_, modT.v()[1]], [hreg(kc, c0, c1)])

            def x_update(pt, n, dm, c0, c1, qg):
                wr = [xreg(dm, c0, c1)]
                if c0 < npr:
                    dve(I("scalar_tensor_tensor", out=xT.h[:, dm, c0:c1], in0=pt.h[:, 0:n],
                                                         scalar=modT.h[:, qg + dm, 0:1], in1=xT.h[:, dm, c0:c1],
                                                         op0=ALU.mult, op1=ALU.add),
                        [pt.v((0, n))[1], modT.v()[1]] + wr, wr)
                else:
                    tmp, tr_ = AR.f32(NRM0 + 1536 + (dm % 2) * 512, (n,))
                    dve(I("tensor_tensor",
                        out=tmp.rearrange("p (s t) -> p s t", t=LS),
                        in0=pt.h[:, 0:n].rearrange("p (s t) -> p s t", t=LS),
                        in1=modT.h[:, qg + dm, 1:1 + NSS].unsqueeze(2).to_broadcast([128, NSS, LS]), op=ALU.mult),
                        [pt.v((0, n))[1], modT.v()[1]], [tr_])
                    dve(I("tensor_tensor", out=xT.h[:, dm, c0:c1], in0=xT.h[:, dm, c0:c1], in1=tmp,
                                                  op=ALU.add), [tr_] + wr, wr)

            def ffn(l, j, sub):
                norm_mod(sub)
                qg = sub * 48 + 32
                G = 8
                HID0 = 0
                SIL0 = (G * TMAX + 1) // 2 + 8
                assert SIL0 + 1024 <= NRM0
                for f0 in range(0, FCN, G):
                    f1 = min(FCN, f0 + G)
                    for fp in range(f0, f1, 2):
                        nf = min(2, f1 - fp)
                        g1, r1 = wload_col(ffn_w1[l, j], fp * 128, nf * 128)
                        g3, r3 = wload_col(ffn_w3[l, j], fp * 128, nf * 128)
                        for fi in range(nf):
                            fc = fp + fi
                            for k, (c0, c1) in enumerate(dtiles_all):
                                n = c1 - c0
                                p1, p3 = dbank(), dbank()
                                hr = [hreg(kc, c0, c1) for kc in range(KC)]
                                P.add("pe", [mm(p1.h[:, 0:n], g1(kc, fi * 128, fi * 128 + 128), hT.h[:, kc, c0:c1],
                                                start=(kc == 0), stop=(kc == KC - 1)) for kc in range(KC)],
                                      [r1] + hr, [p1.v((0, n))[1]])
                                P.add("pe", [mm(p3.h[:, 0:n], g3(kc, fi * 128, fi * 128 + 128), hT.h[:, kc, c0:c1],
                                                start=(kc == 0), stop=(kc == KC - 1)) for kc in range(KC)],
                                      [r3] + hr, [p3.v((0, n))[1]])
                                st_, str_ = AR.f32(SIL0 + ((fc + k) % 2) * 512, (n,))
                                act(st_, p1.h[:, 0:n], AF.Silu, [p1.v((0, n))[1]], [str_])
                                hid, hidr = AR.bf16(HID0 + ((fc - f0) * TMAX) // 2 + c0 // 2, (n,))
                                dve(I("tensor_tensor",
                                    out=hid, in0=st_, in1=p3.h[:, 0:n], op=ALU.mult),
                                    [str_, p3.v((0, n))[1]], [hidr])
                    w2 = []
                    for fp in range(f0, f1, 2):
                        nf = min(2, f1 - fp)
                        w2.append((fp, nf) + wload_row(ffn_w2[l, j], fp * 128, nf * 128))
                    for dm in range(KC):
                        for (c0, c1) in dtiles_all:
                            n = c1 - c0
                            pt = dbank()
                            fns, rds = [], []
                            cnt = f1 - f0
                            i = 0
                            for (fp, nf, g2, r2) in w2:
                                rds.append(r2)
                                for fi in range(nf):
                                    fc = fp + fi
                                    hid, hidr = AR.bf16(HID0 + ((fc - f0) * TMAX) // 2 + c0 // 2, (n,))
                                    rds.append(hidr)
                                    fns.append(mm(pt.h[:, 0:n], g2(fi, dm * 128, dm * 128 + 128), hid,
                                                  start=(i == 0), stop=(i == cnt - 1)))
                                    i += 1
                            P.add("pe", fns, rds, [pt.v((0, n))[1]])
                            x_update(pt, n, dm, c0, c1, qg)

            def final_out():
                tiles = [(yp, t0 + c0, c0, 128) for (c0, c1) in mtiles]
                if has_s:
                    tiles.append((ys, 0, SC0, NS))
                rst = {}
                for k, (c0, c1) in enumerate(dtiles_all):
                    rst[k] = rstd_tile(c0, c1, k) + (c0, c1)
                    rs, rsr, _, _ = rst[k]
                    for i, (dst, r0, cc0, n) in enumerate(tiles):
                        if not (c0 <= cc0 < c1):
                            continue
                        stg, sreg = AR.f32((i % 2) * D, (D,), p=(0, n))
                        for g4 in range(4):
                            pt = dbank()
                            for jj in range(4):
                                kc = g4 * 4 + jj
                                yt, ytr = AR.f32(2 * D + (kc % 4) * 128, (n,))
                                dve(I("scalar_tensor_tensor",
                                    out=yt, in0=xT.h[:, kc, cc0:cc0 + n], scalar=fgT.h[:, kc:kc + 1],
                                    in1=rs[:, cc0 - c0:cc0 - c0 + n], op0=ALU.mult, op1=ALU.mult),
                                    [xreg(kc, cc0, cc0 + n), rsr], [ytr])
                                P.add("pe", tr(pt.h[0:n, jj * 128:(jj + 1) * 128], yt, IDENT[0]), [ytr],
                                      [pt.v((jj * 128, jj * 128 + 128))[1]])
                            P.add("act", I("copy",
                                out=stg[:, g4 * 512:(g4 + 1) * 512], in_=pt.h[0:n, :]), [pt.v()[1]], [sreg])
                        P.dma("sp", dst[r0:r0 + n, :], stg, stslot(), reads=[sreg], final=True)

            load_x()
            for l in range(L):
                layer_params(l)
                if cfg.debug:
                    dve(I("memset", modT.h[:], 0.25), [], [modT.v()[1]])
                elif first:
                    ada(l)
                else:
                    ada_restore(l)
                if not cfg.debug:
                    ffn(l, 0, 0)
                mixer(l, PassCtx, pi, npr, has_s, t0, T, dtiles, dtiles_all, mtiles, SC0, first, last, norm_mod,
                      x_update, xreg, hreg)
                if not cfg.debug:
                    ffn(l, 1, 2)
            final_out()

        def mixer(l, ctx, pi, npr, has_s, t0, T, dtiles, dtiles_all, mtiles, SC0, first, last, norm_mod, x_update,
                  xreg, hreg):
            norm_mod(1)
            QG = 48 + 32
            RAW, QS, KS, VS, GS, SQ = 0, 520, 1032, 1544, 2056, 2568
            SST, OTP, GST = 3080, 3208, 3208 + TMAX
            CTX = GST + 448
            XTR = AW - 1024
            assert CTX + 25 * 128 <= AW and XTR >= CTX + 20 * 128, (XTR, CTX, AW)
            ntile_p = len(mtiles)

            def Cm(i, rows, cols, r0=0):
                return AR.f32(CTX + i * 128, (cols,), p=(r0, r0 + rows))

            def V(off, n, c0=0, p=None):
                return AR.f32(off + c0, (n,), p=p)

            otp_ap = AR.h[:, OTP:OTP + TMAX].bitcast(BF16)
            otp_reg = ("arena", OTP * 4, (OTP + TMAX) * 4)

            def gst(ti, q, n):
                off = GST + ti * 48 + q * 8
                return AR.f32(off, (8,), p=(0, n))

            gab, rab = wload_col(w_in[l], OFF_A, 16)
            alltiles = [(ti, c0, c1, 128, MLE, BONES) for ti, (c0, c1) in enumerate(mtiles)]
            if has_s:
                alltiles.append((ntile_p, SC0, SC0 + NS, NS, MLE4, BONES4))
            Zt = AR.f32(GST + 432, (8,))
            NZt = AR.f32(GST + 440, (8,))
            for (ti, c0, c1, n, mle, bon) in alltiles:
                pt, pc = sbank()
                pab = pt.h[0:n, pc:pc + 16]
                pabr = pt.v((pc, pc + 16))[1]
                P.add("pe", [mm(pab, hT.h[:, kc, c0:c1], gab(kc, 0, 16), start=(kc == 0), stop=(kc == KC - 1))
                             for kc in range(KC)], [rab] + [hreg(kc, c0, c1) for kc in range(KC)], [pabr])
                z, zr = Zt[0][0:n], Zt[1]
                nz, nzr = NZt[0][0:n], NZt[1]
                g_, gr = gst(ti, 0, n)
                nb_, nbr = gst(ti, 1, n)
                gc_, gcr = gst(ti, 2, n)
                bg_, bgr = gst(ti, 3, n)
                kd_, kdr = gst(ti, 4, n)
                be_, ber = gst(ti, 5, n)
                dve(I("tensor_tensor", out=z, in0=pt.h[0:n, pc:pc + 8], in1=rowp.h[0:n, 0, :], op=ALU.add),
                    [pabr, rowp.v()[1]], [zr])
                dve(I("tensor_scalar", out=nz, in0=z, scalar1=-1.0, scalar2=None, op0=ALU.mult), [zr], [nzr])
                dve(I("tensor_tensor", out=nz, in0=nz, in1=z, op=ALU.max), [zr, nzr], [nzr])
                act(nz, nz, AF.Exp, [nzr], [nzr], scale=-1.0)
                act(nz, nz, AF.Ln, [nzr], [nzr], bias=epsc.h[0:n, 1:2])
                dve(I("scalar_tensor_tensor", out=z, in0=z, scalar=0.0, in1=nz, op0=ALU.max, op1=ALU.add),
                    [zr, nzr], [zr])
                dve(I("tensor_tensor", out=g_, in0=z, in1=rowp.h[0:n, 1, :], op=ALU.mult), [zr, rowp.v()[1]], [gr])
                act(be_, pt.h[0:n, pc + 8:pc + 16], AF.Sigmoid, [pabr], [ber])
                dve(I("tensor_scalar", out=nb_, in0=be_, scalar1=-1.0, scalar2=None, op0=ALU.mult), [ber], [nbr])
                p2, pc2 = sbank()
                pgc = p2.h[0:n, pc2:pc2 + 8]
                pgl = p2.h[0:n, pc2 + 8:pc2 + 16]
                pgr = p2.v((pc2, pc2 + 16))[1]
                P.add("pe", [mm(pgc, mle[0][0:n, 0:n], g_), mm(pgl, bon[0][0:n, 0:n], g_)], [gr], [pgr])
                dve(I("tensor_copy", out=gc_, in_=pgc), [pgr], [gcr])
                act(z, pgc, AF.Exp, [pgr], [zr])
                dve(I("tensor_tensor", out=bg_, in0=be_, in1=z, op=ALU.mult), [zr, ber], [bgr])
                dve(I("tensor_tensor", out=nz, in0=pgl, in1=gc_, op=ALU.subtract), [pgr, gcr], [nzr])
                act(kd_, nz, AF.Exp, [nzr], [kdr])

            def conv_piece(ps_t, n, blk, taps, out_off, sample, bias=None, st_src=None, st_col0=0):
                psr = ps_t.v((0, n))[1]
                if not sample:
                    rawf = V(RAW, 3 + n)
                    cc = carry_c.v(l, blk)
                    dve(I("tensor_copy", out=AR.h[:, RAW:RAW + 3], in_=carry_c.h[:, l, blk, :]), [cc[1]], [rawf[1]])
                    P.add("act", I("copy", out=AR.h[:, RAW + 3:RAW + 3 + n], in_=ps_t.h[:, 0:n]), [psr], [rawf[1]])
                    src = lambda j: AR.h[:, RAW + j:RAW + j + n]
                    outv = AR.h[:, out_off:out_off + n]
                else:
                    rawf = V(RAW, NSS * 7)
                    r3 = AR.h[:, RAW:RAW + NSS * 7].rearrange("p (s j) -> p s j", j=7)
                    stg, stgr = AR.f32(XTR, (128,), p=(0, NSS * 3))
                    P.dma("sp", stg, st_src[l][:, st_col0:st_col0 + 128], ldslot(), writes=[stgr])
                    p2, pc2 = sbank()
                    P.add("pe", tr(p2.h[:, pc2:pc2 + NSS * 3], stg, CST.h[0:NSS * 3, 0, 0:NSS * 3]), [stgr],
                          [p2.v((pc2, pc2 + 128))[1]])
                    dve(I("tensor_copy", out=r3[:, :, 0:3],
                          in_=p2.h[:, pc2:pc2 + NSS * 3].rearrange("p (s j) -> p s j", j=3)),
                        [p2.v((pc2, pc2 + 128))[1]], [rawf[1]])
                    P.add("act", I("copy", out=r3[:, :, 3:7],
                                   in_=ps_t.h[:, 0:n].rearrange("p (s t) -> p s t", t=LS)), [psr], [rawf[1]])
                    src = lambda j: r3[:, :, j:j + LS]
                    outv = AR.h[:, out_off:out_off + n].rearrange("p (s t) -> p s t", t=LS)
                outr = ("arena", out_off * 4, (out_off + n) * 4)
                treg = taps[1]
                tp = taps[0]
                if bias is None:
                    dve(I("tensor_scalar", out=outv, in0=src(0), scalar1=tp[:, 0:1], scalar2=None, op0=ALU.mult),
                        [rawf[1], treg], [outr])
                else:
                    dve(I("tensor_scalar", out=outv, in0=src(0), scalar1=tp[:, 0:1], scalar2=bias[0], op0=ALU.mult,
                          op1=ALU.add), [rawf[1], treg, bias[1]], [outr])
                for j in range(1, 4):
                    dve(I("scalar_tensor_tensor", out=outv, in0=src(j), scalar=tp[:, j:j + 1], in1=outv,
                          op0=ALU.mult, op1=ALU.add), [rawf[1], treg, outr], [outr])
                if not sample:
                    dve(I("tensor_copy", out=carry_c.h[:, l, blk, :], in_=AR.h[:, RAW + n:RAW + n + 3]), [rawf[1]],
                        [carry_c.v(l, blk)[1]])
                return outr

            def proj_piece(get, wreg, j0, c0, c1):
                n = c1 - c0
                pt = dbank()
                P.add("pe", [mm(pt.h[:, 0:n], get(kc, j0, j0 + 128), hT.h[:, kc, c0:c1], start=(kc == 0),
                                stop=(kc == KC - 1)) for kc in range(KC)],
                      [wreg] + [hreg(kc, c0, c1) for kc in range(KC)], [pt.v((0, n))[1]])
                return pt

            def conv_state_out(get, wreg, ncols, dst_p, dst_s, col0):
                jobs = []
                if last:
                    jobs.append(("p", npr - 4, npr, 4))
                if has_s:
                    jobs.append(("s", SC0, SC0 + NS, NS))
                for (kind, c0, c1, m) in jobs:
                    pt = dbank()
                    P.add("pe", [mm(pt.h[0:m, 0:ncols], hT.h[:, kc, c0:c1], get(kc, 0, ncols), start=(kc == 0),
                                    stop=(kc == KC - 1)) for kc in range(KC)],
                          [wreg] + [hreg(kc, c0, c1) for kc in range(KC)], [pt.v((0, ncols))[1]])
                    stg, stgr = AR.f32(XTR + 256, (ncols,), p=(0, m))
                    dve(I("tensor_copy", out=stg, in_=pt.h[0:m, 0:ncols]), [pt.v((0, ncols))[1]], [stgr])
                    if kind == "p":
                        P.dma("sp", dst_p[l, :, col0:col0 + ncols], AR.h[1:4, XTR + 256:XTR + 256 + ncols], stslot(),
                              reads=[stgr], final=True)
                    else:
                        for t in range(1, LS):
                            P.dma("sp", dst_s[l, :, t - 1, col0:col0 + ncols],
                                  AR.h[t:NS:LS, XTR + 256:XTR + 256 + ncols], stslot(), reads=[stgr], final=True)

            def l2norm(off, n, scale):
                xr = ("arena", off * 4, (off + n) * 4)
                sq, sqr = V(SQ, n)
                xv = AR.h[:, off:off + n]
                dve(I("tensor_tensor", out=sq, in0=xv, in1=xv, op=ALU.mult), [xr], [sqr])
                pt = dbank()
                P.add("pe", mm(pt.h[:, 0:n], ONESF[0], sq), [sqr], [pt.v((0, n))[1]])
                act(sq, pt.h[:, 0:n], AF.Sqrt, [pt.v((0, n))[1]], [sqr], bias=epsc.h[:, 0:1], scale=1.0)
                dve(I("reciprocal", out=sq, in_=sq), [sqr], [sqr])
                if scale is None:
                    dve(I("tensor_tensor", out=xv, in0=xv, in1=sq, op=ALU.mult), [xr, sqr], [xr])
                else:
                    dve(I("scalar_tensor_tensor", out=xv, in0=xv, scalar=scale, in1=sq, op0=ALU.mult, op1=ALU.mult),
                        [xr, sqr], [xr])

            def prep(h, ti, n, a0, mle, ml, par):
                nlev = 5 if n == 128 else 1
                R_ = {}
                ks = AR.h[:, KS + a0:KS + a0 + n]
                ksr = ("arena", (KS + a0) * 4, (KS + a0 + n) * 4)
                qs = AR.h[:, QS + a0:QS + a0 + n]
                qsr = ("arena", (QS + a0) * 4, (QS + a0 + n) * 4)
                vs = AR.h[:, VS + a0:VS + a0 + n]
                vsr = ("arena", (VS + a0) * 4, (VS + a0 + n) * 4)
                col = lambda q: (AR.h[0:n, GST + ti * 48 + q * 8 + h:GST + ti * 48 + q * 8 + h + 1], gst(ti, q, n)[1])
                g_, nb_, gc_, bg_, kd_, be_ = [col(q) for q in range(6)]
                B0 = 15 + par * 5
                kbg = Cm(0, n, 128)
                vb = Cm(1, n, 128)
                gm = Cm(2, n, 128)
                dm_ = Cm(3, n, n)
                dt_ = Cm(4, n, n)
                e1 = Cm(5, 128, n)
                mats = [Cm(6, n, n), Cm(7, n, n), Cm(9, n, n), Cm(10, n, n)]
                y_ = Cm(8, n, n)
                wT = Cm(B0 + 0, 128, n)
                u_ = Cm(B0 + 1, n, 128)
                qkT = Cm(B0 + 2, n, n)
                kdec = Cm(B0 + 3, n, 128)
                qdT = Cm(B0 + 4, 128, n)
                pk, pkc = sbank()
                pkr = pk.v((pkc, pkc + 128))[1]
                P.add("pe", tr(pk.h[0:n, pkc:pkc + 128], ks, IDENT[0]), [ksr], [pkr])
                pv, pvc = sbank()
                pvr = pv.v((pvc, pvc + 128))[1]
                P.add("pe", tr(pv.h[0:n, pvc:pvc + 128], vs, IDENT[0]), [vsr], [pvr])
                if cfg.debug == 511:
                    return R_
                dve(I("tensor_scalar", out=kbg[0], in0=pk.h[0:n, pkc:pkc + 128], scalar1=bg_[0], scalar2=None,
                      op0=ALU.mult), [pkr, bg_[1]], [kbg[1]])
                if cfg.debug == 512:
                    return R_
                dve(I("tensor_scalar", out=kdec[0], in0=pk.h[0:n, pkc:pkc + 128], scalar1=kd_[0], scalar2=None,
                      op0=ALU.mult), [pkr, kd_[1]], [kdec[1]])
                dve(I("tensor_scalar", out=vb[0], in0=pv.h[0:n, pvc:pvc + 128], scalar1=be_[0], scalar2=None,
                      op0=ALU.mult), [pvr, be_[1]], [vb[1]])
                if cfg.debug == 51:
                    return R_
                yield
                dve(I("tensor_scalar", out=gm[0], in0=ONESF[0][0:n, :], scalar1=g_[0], scalar2=None, op0=ALU.mult),
                    [g_[1]], [gm[1]])
                pg, pgc = sbank()
                pgr = pg.v((pgc, pgc + 128))[1]
                P.add("pe", mm(pg.h[:, pgc:pgc + n], gm[0], mle[0][0:n, 0:n]), [gm[1]], [pgr])
                yield
                dve(I("tensor_scalar", out=dm_[0], in0=pg.h[0:n, pgc:pgc + n], scalar1=gc_[0], scalar2=0.0,
                      op0=ALU.subtract, op1=ALU.max), [pgr, gc_[1]], [dm_[1]])
                act(dm_[0], dm_[0], AF.Exp, [dm_[1]], [dm_[1]], scale=-1.0)
                dve(I("tensor_tensor", out=dm_[0], in0=dm_[0], in1=ml[0][0:n, 0:n], op=ALU.mult), [dm_[1]], [dm_[1]])
                dve(I("tensor_scalar", out=dt_[0], in0=pg.h[0:n, pgc:pgc + n], scalar1=gc_[0], scalar2=0.0,
                      op0=ALU.subtract, op1=ALU.min), [pgr, gc_[1]], [dt_[1]])
                act(dt_[0], dt_[0], AF.Exp, [dt_[1]], [dt_[1]])
                dve(I("tensor_tensor", out=dt_[0], in0=dt_[0], in1=mle[0][0:n, 0:n], op=ALU.mult), [dt_[1]],
                    [dt_[1]])
                act(e1[0], pg.h[:, pgc:pgc + n], AF.Exp, [pgr], [e1[1]])
                dve(I("tensor_tensor", out=qdT[0], in0=qs, in1=e1[0], op=ALU.mult), [qsr, e1[1]], [qdT[1]])
                if n == 128:
                    egl = AR.f32(CTX + 11 * 128 + par * 64, (2,))
                    act(egl[0], pg.h[:, pgc + 63:pgc + 128:64], AF.Exp, [pgr], [egl[1]])
                else:
                    egl = AR.f32(CTX + 11 * 128 + par * 64, (NSS,))
                    act(egl[0], pg.h[:, pgc + LS - 1:pgc + n:LS], AF.Exp, [pgr], [egl[1]])
                R_["egl"] = egl
                if cfg.debug == 52:
                    return R_
                yield
                pkk, pkkc = sbank()
                pkkr = pkk.v((pkkc, pkkc + 128))[1]
                P.add("pe", mm(pkk.h[0:n, pkkc:pkkc + n], ks, ks), [ksr], [pkkr])
                pkq, pkqc = sbank()
                pkqr = pkq.v((pkqc, pkqc + 128))[1]
                P.add("pe", mm(pkq.h[0:n, pkqc:pkqc + n], ks, qs), [ksr, qsr], [pkqr])
                Bm, Am, A2m, B2m = mats
                dve(I("scalar_tensor_tensor", out=Bm[0], in0=pkk.h[0:n, pkkc:pkkc + n], scalar=nb_[0], in1=dm_[0],
                      op0=ALU.mult, op1=ALU.mult), [pkkr, nb_[1], dm_[1]], [Bm[1]])
                dve(I("tensor_tensor", out=qkT[0], in0=pkq.h[0:n, pkqc:pkqc + n], in1=dt_[0], op=ALU.mult),
                    [pkqr, dt_[1]], [qkT[1]])
                if cfg.debug == 53:
                    return R_
                yield
                pa, pac = sbank()
                par_ = pa.v((pac, pac + 128))[1]
                P.add("pe", tr(pa.h[0:n, pac:pac + n], Bm[0], IDENT[0][0:n, 0:n]), [Bm[1]], [par_])
                dve(I("tensor_copy", out=Am[0], in_=pa.h[0:n, pac:pac + n]), [par_], [Am[1]])
                dve(I("tensor_tensor", out=y_[0], in0=pa.h[0:n, pac:pac + n], in1=IDENT[0][0:n, 0:n], op=ALU.add),
                    [par_], [y_[1]])
                if cfg.debug == 54:
                    return R_
                yield
                curA, curB, nxtA, nxtB = Am, Bm, A2m, B2m
                for lev in range(nlev):
                    lastlev = lev == nlev - 1
                    pb, pbc = sbank()
                    pbr = pb.v((pbc, pbc + 128))[1]
                    P.add("pe", mm(pb.h[0:n, pbc:pbc + n], curA[0], curB[0]), [curA[1], curB[1]], [pbr])
                    dve(I("tensor_copy", out=nxtB[0], in_=pb.h[0:n, pbc:pbc + n]), [pbr], [nxtB[1]])
                    if not lastlev:
                        pa2, pa2c = sbank()
                        pa2r = pa2.v((pa2c, pa2c + 128))[1]
                        P.add("pe", mm(pa2.h[0:n, pa2c:pa2c + n], curB[0], curA[0]), [curA[1], curB[1]], [pa2r])
                        dve(I("tensor_copy", out=nxtA[0], in_=pa2.h[0:n, pa2c:pa2c + n]), [pa2r], [nxtA[1]])
                    py, pyc = sbank()
                    pyr = py.v((pyc, pyc + 128))[1]
                    P.add("pe", mm(py.h[0:n, pyc:pyc + n], nxtB[0], y_[0]), [nxtB[1], y_[1]], [pyr])
                    dve(I("tensor_tensor", out=y_[0], in0=py.h[0:n, pyc:pyc + n], in1=y_[0], op=ALU.add),
                        [pyr, y_[1]], [y_[1]])
                    curA, curB, nxtA, nxtB = nxtA, nxtB, curA, curB
                    yield
                if cfg.debug == 55:
                    return R_
                yield
                pw, pwc = sbank()
                pwr = pw.v((pwc, pwc + 128))[1]
                P.add("pe", mm(pw.h[:, pwc:pwc + n], kbg[0], y_[0]), [kbg[1], y_[1]], [pwr])
                dve(I("tensor_copy", out=wT[0], in_=pw.h[:, pwc:pwc + n]), [pwr], [wT[1]])
                pu, puc = sbank()
                pur = pu.v((puc, puc + 128))[1]
                P.add("pe", mm(pu.h[0:n, puc:puc + 128], y_[0], vb[0]), [vb[1], y_[1]], [pur])
                dve(I("tensor_copy", out=u_[0], in_=pu.h[0:n, puc:puc + 128]), [pur], [u_[1]])
                R_.update(wT=wT, u=u_, qkT=qkT, kdec=kdec, qdT=qdT)
                return R_

            def o_finish(o_, n, a0, hh, ocol0):
                osq = Cm(14, n, 128)
                ss = AR.f32(CTX + 11 * 128 + 40, (1,), p=(0, n))
                dve(I("tensor_tensor", out=osq[0], in0=o_[0], in1=o_[0], op=ALU.mult), [o_[1]], [osq[1]])
                dve(I("reduce_sum", out=ss[0], in_=osq[0], axis=AX.X), [osq[1]], [ss[1]])
                act(ss[0], ss[0], AF.Sqrt, [ss[1]], [ss[1]], bias=epsc.h[0:n, 0:1], scale=1.0 / DV)
                dve(I("reciprocal", out=ss[0], in_=ss[0]), [ss[1]], [ss[1]])
                dve(I("tensor_scalar", out=osq[0], in0=o_[0], scalar1=ss[0], scalar2=None, op0=ALU.mult),
                    [o_[1], ss[1]], [osq[1]])
                po, poc = sbank()
                por = po.v((poc, poc + 128))[1]
                P.add("pe", tr(po.h[:, poc:poc + n], osq[0], IDENT[0][0:n, 0:n]), [osq[1]], [por])
                gsr = ("arena", (GS + a0) * 4, (GS + a0 + n) * 4)
                dve(I("scalar_tensor_tensor", out=otp_ap[:, hh * TMAX + ocol0:hh * TMAX + ocol0 + n],
                      in0=po.h[:, poc:poc + n], scalar=onT.h[:, 0:1], in1=AR.h[:, GS + a0:GS + a0 + n],
                      op0=ALU.mult, op1=ALU.mult), [por, gsr, onT.v()[1]], [otp_reg])

            S_ = V(SST, 128)

            def recur_prompt(R_, n, a0, hh, ocol0):
                up = Cm(12, n, 128)
                o_ = Cm(13, n, 128)
                for c in range(n // 64):
                    r0 = 64 * c
                    pw, pwc = sbank()
                    pwr = pw.v((pwc, pwc + 128))[1]
                    P.add("pe", mm(pw.h[0:n, pwc:pwc + 128], R_["wT"][0], S_[0]), [R_["wT"][1], S_[1]], [pwr])
                    dve(I("tensor_tensor", out=up[0][r0:r0 + 64, :], in0=R_["u"][0][r0:r0 + 64, :],
                          in1=pw.h[r0:r0 + 64, pwc:pwc + 128], op=ALU.subtract), [pwr, R_["u"][1]], [up[1]])
                    yield
                    po, poc = sbank()
                    por = po.v((poc, poc + 128))[1]
                    P.add("pe", [mm(po.h[0:n, poc:poc + 128], R_["qdT"][0], S_[0], start=True, stop=False),
                                 mm(po.h[0:n, poc:poc + 128], R_["qkT"][0][r0:r0 + 64, :], up[0][r0:r0 + 64, :],
                                    start=False, stop=True)],
                          [R_["qdT"][1], S_[1], R_["qkT"][1], up[1]], [por])
                    dve(I("tensor_copy", out=o_[0][r0:r0 + 64, :], in_=po.h[r0:r0 + 64, poc:poc + 128]), [por], [o_[1]])
                    ps_, psc = sbank()
                    psr_ = ps_.v((psc, psc + 128))[1]
                    P.add("pe", mm(ps_.h[:, psc:psc + 128], R_["kdec"][0][r0:r0 + 64, :], up[0][r0:r0 + 64, :]),
                          [R_["kdec"][1], up[1]], [psr_])
                    dve(I("scalar_tensor_tensor", out=S_[0], in0=S_[0], scalar=R_["egl"][0][:, c:c + 1],
                          in1=ps_.h[:, psc:psc + 128], op0=ALU.mult, op1=ALU.add), [psr_, R_["egl"][1], S_[1]],
                        [S_[1]])
                    yield
                o_finish(o_, n, a0, hh, ocol0)
                yield

            def recur_sample(R_, h, hh, ocol0):
                n = NS
                NBS = 2
                up = Cm(12, n, 128)
                o_ = Cm(13, n, 128)
                sbuf_ = [AR.f32(XTR + 512 + i * 256, (NBS, 128)) for i in range(2)]
                wTm = AR.f32(XTR + 0, (NBS, n))
                qdm = AR.f32(XTR + 128, (NBS, n))
                kdm = AR.f32(XTR + 256, (NBS, 128), p=(0, n))
                pw, pwc = sbank()
                pwr = pw.v((pwc, pwc + 128))[1]
                po, poc = sbank()
                por = po.v((poc, poc + 128))[1]
                nbt = NSS // NBS
                for b in range(nbt):
                    s0 = b * NBS
                    sb = sbuf_[b % 2]
                    P.dma("sp", sb[0], sd[l, s0:s0 + NBS, h].rearrange("s k v -> k s v"), ldslot(), writes=[sb[1]])
                    dve(I("tensor_tensor", out=wTm[0], in0=R_["wT"][0].unsqueeze(1).to_broadcast([128, NBS, n]),
                          in1=seqrowT.h[:, s0:s0 + NBS, :], op=ALU.mult), [R_["wT"][1]], [wTm[1]])
                    dve(I("tensor_tensor", out=qdm[0], in0=R_["qdT"][0].unsqueeze(1).to_broadcast([128, NBS, n]),
                          in1=seqrowT.h[:, s0:s0 + NBS, :], op=ALU.mult), [R_["qdT"][1]], [qdm[1]])
                    for i in range(NBS):
                        s = s0 + i
                        P.add("pe", mm(pw.h[0:n, pwc:pwc + 128], wTm[0][:, i, :], sb[0][:, i, :], start=(s == 0),
                                       stop=(s == NSS - 1)), [wTm[1], sb[1]], [pwr])
                        P.add("pe", mm(po.h[0:n, poc:poc + 128], qdm[0][:, i, :], sb[0][:, i, :], start=(s == 0),
                                       stop=False), [qdm[1], sb[1]], [por])
                dve(I("tensor_tensor", out=up[0], in0=R_["u"][0], in1=pw.h[0:n, pwc:pwc + 128], op=ALU.subtract),
                    [pwr, R_["u"][1]], [up[1]])
                P.add("pe", mm(po.h[0:n, poc:poc + 128], R_["qkT"][0], up[0], start=False, stop=True),
                      [R_["qkT"][1], up[1]], [por])
                dve(I("tensor_copy", out=o_[0], in_=po.h[0:n, poc:poc + 128]), [por], [o_[1]])
                for b in range(nbt):
                    s0 = b * NBS
                    sb = sbuf_[b % 2]
                    P.dma("sp", sb[0], sd[l, s0:s0 + NBS, h].rearrange("s k v -> k s v"), ldslot(), writes=[sb[1]])
                    dve(I("tensor_tensor", out=kdm[0], in0=R_["kdec"][0].unsqueeze(1).to_broadcast([n, NBS, 128]),
                          in1=seqcolT.h[:, s0:s0 + NBS].unsqueeze(2).to_broadcast([n, NBS, 128]), op=ALU.mult),
                        [R_["kdec"][1]], [kdm[1]])
                    for i in range(NBS):
                        s = s0 + i
                        ps_, psc = sbank()
                        psr_ = ps_.v((psc, psc + 128))[1]
                        P.add("pe", mm(ps_.h[:, psc:psc + 128], kdm[0][:, i, :], up[0]), [kdm[1], up[1]], [psr_])
                        dve(I("scalar_tensor_tensor", out=sb[0][:, i, :], in0=sb[0][:, i, :],
                              scalar=R_["egl"][0][:, s:s + 1], in1=ps_.h[:, psc:psc + 128], op0=ALU.mult,
                              op1=ALU.add), [psr_, R_["egl"][1], sb[1]], [sb[1]])
                    P.dma("sp", sd_o[l, s0:s0 + NBS, h].rearrange("s k v -> k s v"), sb[0], stslot(), reads=[sb[1]],
                          final=True)
                o_finish(o_, n, 0, hh, ocol0)

            def wout_pair(row0):
                gw, rw = wload_row(w_out[l], row0, 256)
                for dm in range(KC):
                    for (c0, c1) in dtiles_all:
                        n = c1 - c0
                        pt = dbank()
                        P.add("pe", [mm(pt.h[:, 0:n], gw(a, dm * 128, dm * 128 + 128),
                                        otp_ap[:, a * TMAX + c0:a * TMAX + c1], start=(a == 0), stop=(a == 1))
                                     for a in range(2)], [rw, otp_reg], [pt.v((0, n))[1]])
                        x_update(pt, n, dm, c0, c1, QG)

            par = [0]
            pend = [None]

            def drain():
                if pend[0] is not None:
                    for _ in pend[0]:
                        pass
                    pend[0] = None
            if cfg.debug >= 4:
                dve(I("memset", AR.h[:, OTP:OTP + TMAX], 0.0), [], [otp_reg])
            for hp in range(H // 2 if cfg.debug != 3 else 0):
                gq, rq = wload_col(w_in[l], OFF_Q + hp * 256, 256)
                gk, rk = wload_col(w_in[l], OFF_K + hp * 256, 256)
                gv, rv = wload_col(w_in[l], OFF_V + hp * 256, 256)
                gg, rg = wload_col(w_in[l], OFF_G + hp * 256, 256)
                conv_state_out(gq, rq, 256, pdc, sdc_o, OFF_Q + hp * 256)
                conv_state_out(gk, rk, 256, pdc, sdc_o, OFF_K + hp * 256)
                conv_state_out(gv, rv, 256, pdc, sdc_o, OFF_V + hp * 256)
                for hh in range(2):
                    h = hp * 2 + hh
                    if first:
                        dve(I("memset", S_[0], 0.0), [], [S_[1]])
                    else:
                        P.dma("sp", S_[0], sscr[l, h], ldslot(), writes=[S_[1]])
                    for (c0, c1) in dtiles_all:
                        n = c1 - c0
                        sample = c0 >= npr
                        for (get, wreg, sec, off) in ((gq, rq, 0, QS), (gk, rk, 1, KS), (gv, rv, 2, VS)):
                            pt = proj_piece(get, wreg, hh * 128, c0, c1)
                            blk = sec * 8 + h
                            outr = conv_piece(pt, n, blk, (dcwT.h[:, blk, :], dcwT.v()[1]), off, sample,
                                              st_src=sdc, st_col0=blk * 128)
                            act(AR.h[:, off:off + n], AR.h[:, off:off + n], AF.Silu, [outr], [outr])
                        pt = proj_piece(gg, rg, hh * 128, c0, c1)
                        act(AR.h[:, GS:GS + n], pt.h[:, 0:n], AF.Silu, [pt.v((0, n))[1]],
                            [("arena", GS * 4, (GS + n) * 4)])
                        l2norm(QS, n, float(DK) ** -0.5)
                        l2norm(KS, n, None)
                        dbg = cfg.debug
                        if dbg == 4:
                            continue
                        if dbg >= 50:
                            dbg = 5
                        if not sample:
                            for a0 in range(0, n, 128):
                                ti = (c0 + a0) // 128
                                gp = prep(h, ti, 128, a0, MLE, ML, par[0] % 2)
                                par[0] += 1
                                R_ = None
                                while True:
                                    try:
                                        next(gp)
                                    except StopIteration as e_:
                                        R_ = e_.value
                                        break
                                    if pend[0] is not None:
                                        try:
                                            next(pend[0])
                                        except StopIteration:
                                            pend[0] = None
                                drain()
                                if dbg != 5:
                                    pend[0] = recur_prompt(R_, 128, a0, hh, c0 + a0)
                            drain()
                        else:
                            drain()
                            gp = prep(h, ntile_p, NS, 0, MLE4, ML4, 0)
                            R_ = None
                            while True:
                                try:
                                    next(gp)
                                except StopIteration as e_:
                                    R_ = e_.value
                                    break
                            if dbg not in (5, 6):
                                recur_sample(R_, h, hh, c0)
                    drain()
                    if last:
                        P.dma("sp", pd[l, h], S_[0], stslot(), reads=[S_[1]], final=True)
                    else:
                        P.dma("sp", sscr[l, h], S_[0], stslot(), reads=[S_[1]], writes=[("sscr", (l * H + h) * 4, (l * H + h) * 4 + 4)])
                wout_pair(hp * 256)

            def lru_wload(bp):
                def pa(s):
                    return (ring.h[:, s, 0:256].rearrange("p (n j) -> p n j", j=128),
                            lru_wa[l, 2 * bp:2 * bp + 2].rearrange("n i j -> i n j"))

                def pb(s):
                    return (ring.h[:, s, 256:512].rearrange("p (n j) -> p n j", j=128),
                            lru_wx[l, 2 * bp:2 * bp + 2].rearrange("n i j -> i n j"))
                s = wload([pa, pb])
                return s, ring.v(s)[1]
            XC, RR, II, HH_, XB = QS, KS, VS, GS, SQ
            for bp in range(NB // 2 if cfg.debug in (0, 1, 3) else 0):
                gx, rx = wload_col(w_in[l], OFF_X + bp * 256, 256)
                gy, ry = wload_col(w_in[l], OFF_Y + bp * 256, 256)
                sl_, rlw = lru_wload(bp)
                conv_state_out(gx, rx, 256, plc, slc_o, bp * 256)
                for bb in range(2):
                    nb = bp * 2 + bb
                    blk = 24 + nb
                    for (c0, c1) in dtiles_all:
                        n = c1 - c0
                        sample = c0 >= npr
                        pt = proj_piece(gx, rx, bb * 128, c0, c1)
                        xcr = conv_piece(pt, n, blk, (lcwT.h[:, nb, :], lcwT.v()[1]), XC, sample,
                                         bias=(lprm.h[:, 0, nb:nb + 1], lprm.v()[1]), st_src=slc, st_col0=nb * 128)
                        xc = AR.h[:, XC:XC + n]
                        xb, xbr = AR.bf16(RAW, (n,))
                        P.add("act", I("copy", out=xb, in_=xc), [xcr], [xbr])
                        pr = dbank()
                        P.add("pe", mm(pr.h[:, 0:n], ring.h[:, sl_, bb * 128:(bb + 1) * 128], xb), [rlw, xbr],
                              [pr.v((0, n))[1]])
                        pi_ = dbank()
                        P.add("pe", mm(pi_.h[:, 0:n], ring.h[:, sl_, (2 + bb) * 128:(2 + bb + 1) * 128], xb),
                              [rlw, xbr], [pi_.v((0, n))[1]])
                        rr, rrr = V(RR, n)
                        ii, iir = V(II, n)
                        hhv, hhr = V(HH_, n)
                        sq, sqr = V(XB, n)
                        act(rr, pr.h[:, 0:n], AF.Sigmoid, [pr.v((0, n))[1], lprm.v()[1]], [rrr],
                            bias=lprm.h[:, 1, nb:nb + 1])
                        act(rr, rr, AF.Exp, [rrr, lprm.v()[1]], [rrr], scale=lprm.h[:, 3, nb:nb + 1])
                        act(ii, pi_.h[:, 0:n], AF.Sigmoid, [pi_.v((0, n))[1], lprm.v()[1]], [iir],
                            bias=lprm.h[:, 2, nb:nb + 1])
                        dve(I("tensor_tensor", out=ii, in0=ii, in1=xc, op=ALU.mult), [iir, xcr], [iir])
                        dve(I("tensor_tensor", out=sq, in0=rr, in1=rr, op=ALU.mult), [rrr], [sqr])
                        act(sq, sq, AF.Sqrt, [sqr], [sqr], bias=epsc.h[:, 1:2], scale=-1.0)
                        dve(I("tensor_tensor", out=ii, in0=ii, in1=sq, op=ALU.mult), [iir, sqr], [iir])
                        if not sample:
                            dve(I("tensor_tensor_scan", out=hhv, data0=rr, data1=ii,
                                  initial=carry_h.h[:, l, nb:nb + 1], op0=ALU.mult, op1=ALU.add),
                                [rrr, iir, carry_h.v()[1]], [hhr])
                            dve(I("tensor_copy", out=carry_h.h[:, l, nb:nb + 1], in_=AR.h[:, HH_ + n - 1:HH_ + n]),
                                [hhr], [carry_h.v()[1]])
                            if last and c1 == npr:
                                P.dma("sp", pl[l, nb * 128:(nb + 1) * 128].rearrange("(p o) -> p o", o=1),
                                      AR.h[:, HH_ + n - 1:HH_ + n], stslot(), reads=[hhr], final=True)
                        else:
                            stg, stgr = AR.f32(XTR, (128,), p=(0, NSS))
                            P.dma("sp", stg, sl[l][:, nb * 128:(nb + 1) * 128], ldslot(), writes=[stgr])
                            p2, pc2 = sbank()
                            p2r = p2.v((pc2, pc2 + 128))[1]
                            P.add("pe", tr(p2.h[:, pc2:pc2 + NSS], stg, CST.h[0:NSS, 0, 0:NSS]), [stgr], [p2r])
                            h3 = hhv.rearrange("p (s t) -> p s t", t=LS)
                            a3 = rr.rearrange("p (s t) -> p s t", t=LS)
                            b3 = ii.rearrange("p (s t) -> p s t", t=LS)
                            for t in range(LS):
                                prev = p2.h[:, pc2:pc2 + NSS] if t == 0 else h3[:, :, t - 1]
                                rd = [p2r] if t == 0 else []
                                dve(I("tensor_tensor", out=h3[:, :, t], in0=a3[:, :, t], in1=prev, op=ALU.mult),
                                    [rrr, hhr] + rd, [hhr])
                                dve(I("tensor_tensor", out=h3[:, :, t], in0=h3[:, :, t], in1=b3[:, :, t],
                                      op=ALU.add), [iir, hhr], [hhr])
                            p3, pc3 = sbank()
                            p3r = p3.v((pc3, pc3 + 128))[1]
                            hl, hlr = AR.f32(XTR + 128, (NSS,))
                            dve(I("tensor_copy", out=hl, in_=h3[:, :, LS - 1]), [hhr], [hlr])
                            P.add("pe", tr(p3.h[0:NSS, pc3:pc3 + 128], hl, IDENT[0]), [hlr], [p3r])
                            so, sor = AR.f32(XTR + 256, (128,), p=(0, NSS))
                            dve(I("tensor_copy", out=so, in_=p3.h[0:NSS, pc3:pc3 + 128]), [p3r], [sor])
                            P.dma("sp", sl_o[l][:, nb * 128:(nb + 1) * 128], so, stslot(), reads=[sor], final=True)
                        py = proj_piece(gy, ry, bb * 128, c0, c1)
                        act(sq, py.h[:, 0:n], AF.Gelu_apprx_tanh, [py.v((0, n))[1]], [sqr])
                        dve(I("tensor_tensor", out=otp_ap[:, bb * TMAX + c0:bb * TMAX + c1], in0=hhv, in1=sq,
                              op=ALU.mult), [hhr, sqr], [otp_reg])
                wout_pair((8 + bp * 2) * 128)

        setup()
        for pi in range(len(cfg.passes)):
            run_pass(pi)

        block = es.enter_context(nc.Block())
        P.emit(nc, block, esem)
    return nc


def make_consts():
    c = np.zeros((128, 8, 128), np.float32)
    i = np.arange(128)
    c[:, 0, :] = np.eye(128)
    c[:, 1, :] = 1.0
    same64 = (i[:, None] // 64) == (i[None, :] // 64)
    c[:, 2, :] = same64 & (i[:, None] <= i[None, :])
    c[:, 3, :] = same64 & (i[:, None] > i[None, :])
    c[:, 4, :] = same64
    same4 = ((i[:, None] // 4) == (i[None, :] // 4)) & (i[:, None] < 64) & (i[None, :] < 64)
    c[:, 5, :] = same4 & (i[:, None] <= i[None, :])
    c[:, 6, :] = same4 & (i[:, None] > i[None, :])
    c[:, 7, :] = same4
    t = np.arange(NS)
    seqrow = (t[None, :] // LS == np.arange(NSS)[:, None]).astype(np.float32).reshape(-1)
    seqcol = (t[:, None] // LS == np.arange(NSS)[None, :]).astype(np.float32)
    return c, seqrow, seqcol


_PROG_CACHE = {}


def run_cfg(cfg, inp, n_cores=8, n_prompt=4):
    key = (cfg.depth, cfg.passes, cfg.debug)
    if key not in _PROG_CACHE:
        _PROG_CACHE[key] = build_program(cfg)
    nc = _PROG_CACHE[key]
    cst, seqrow, seqcol = make_consts()
    f = lambda a: np.ascontiguousarray(np.asarray(a, dtype=np.float32))
    shared = {k: f(inp[k]) for k in ("w_in", "w_out", "norm_g", "w_ada", "b_ada", "ffn_w1", "ffn_w3", "ffn_w2",
                                     "dconv_w", "d_alog", "d_dtbias", "d_onorm", "lconv_w", "lconv_b", "lru_wa",
                                     "lru_ba", "lru_wx", "lru_bx", "lru_lam", "final_g")}
    shared.update(cst=cst, seqrow=seqrow, seqcol=seqcol)
    if cfg.debug:
        for k_ in ("w_ada", "ffn_w1", "ffn_w3", "ffn_w2"):
            shared[k_] = np.zeros((1, 128, 128) if k_ == "w_ada" else (1, 1, 128, 128), np.float32)
    Ld = cfg.depth
    in_maps = []
    for c in range(n_cores):
        p = c % n_prompt
        s0, s1 = c * NSS, (c + 1) * NSS
        m = dict(shared)
        m["xp"] = f(inp["x_prompt"][p])
        m["xs"] = f(inp["x_sample"][s0:s1]).reshape(NS, D)
        m["c"] = f(np.concatenate([inp["c_prompt"][p:p + 1], inp["c_sample"][s0:s1]], axis=0))
        m["sd"] = f(inp["state_delta"][:, s0:s1])
        m["sdc"] = f(inp["state_delta_conv"][:, s0:s1]).reshape(Ld, NSS * 3, NDC)
        m["sl"] = f(inp["state_lru"][:, s0:s1])
        m["slc"] = f(inp["state_lru_conv"][:, s0:s1]).reshape(Ld, NSS * 3, WB)
        in_maps.append(m)
    res = run_bass_kernel_spmd(nc, in_maps, core_ids=list(range(n_cores)))
    return res.results


def assemble(results, n_prompt=4):
    cat = lambda k, ax: np.concatenate([r[k] for r in results], axis=ax)
    stack_p = lambda k: np.stack([results[p][k] for p in range(n_prompt)], axis=1)
    y_prompt = np.stack([results[p]["yp"] for p in range(n_prompt)], axis=0)
    y_sample = cat("ys", 0).reshape(len(results) * NSS, LS, D)
    return (y_prompt, y_sample, stack_p("pd"), stack_p("pdc"), stack_p("pl"), stack_p("plc"),
            cat("sd_o", 1), cat("sdc_o", 1), cat("sl_o", 1), cat("slc_o", 1))


def kernel(**inputs):
    cfg = Cfg()
    results = run_cfg(cfg, inputs)
    return tuple(np.ascontiguousarray(a, dtype=np.float32) for a in assemble(results))
```
